# Optimizing a Trainium2 kernel written in Bass

```python
import math
import jax, jax.numpy as jnp
from jax import lax
import numpy as np

D_MODEL = 4096
BATCH = 4
SEQ = 4096
DEPTH = 2

CTX_LEN = 256
GRID_W = 64
Q_BLOCK = 128
ROPE_THETA = 10000.0
NORM_EPS = 1e-6
MLP_HIDDEN = 4 * D_MODEL
N_MOD = 6
HALF = D_MODEL // 2

RW_HEAD = 64
RW_W = HALF
RW_HEADS = RW_W // RW_HEAD
DECAY_RANK = 96
AAA_RANK = 96
GATE_RANK = 256
GN_EPS = 64e-5
RW_IN = 3 * RW_W + GATE_RANK + 2 * DECAY_RANK + 2 * AAA_RANK

GQ_HEAD = 128
GQ_HEADS = HALF // GQ_HEAD
GQ_KV_HEADS = GQ_HEADS // 4
GQ_GROUP = GQ_HEADS // GQ_KV_HEADS
GQ_SCALE = GQ_HEAD ** -0.5
IN_EVEN = RW_IN + (GQ_HEADS + 2 * GQ_KV_HEADS) * GQ_HEAD

MLA_NOPE = 128
MLA_ROPE = 64
MLA_V = 128
MLA_HEADS = HALF // MLA_V
Q_LORA = 768
KV_LORA = 512
MLA_SCALE = (MLA_NOPE + MLA_ROPE) ** -0.5
MLA_IN = Q_LORA + KV_LORA + MLA_ROPE

DIFF_HEAD = 64
DIFF_V = 2 * DIFF_HEAD
DIFF_HEADS = HALF // DIFF_V
DIFF_SCALE = DIFF_HEAD ** -0.5
DIFF_IN = DIFF_HEADS * (4 * DIFF_HEAD + DIFF_V)
IN_ODD = MLA_IN + DIFF_IN

N_EVEN = (DEPTH + 1) // 2
N_ODD = DEPTH // 2

kernel_name = "hybrid_rwkv7_gqa_mla_diffattn_dit_trunk"


def _rms(x, g):
    xf = x.astype(jnp.float32)
    y = xf * lax.rsqrt(jnp.mean(xf * xf, axis=-1, keepdims=True) + NORM_EPS)
    return (y * g.astype(jnp.float32)).astype(x.dtype)


def _mlp(h, w1, w2):
    return jnp.square(jax.nn.relu(h @ w1)) @ w2


def _heads(z, n_heads):
    return z.reshape(z.shape[:-1] + (n_heads, z.shape[-1] // n_heads))


def _axial_rope_tables(n_tokens, rot_dim):
    n_rows = n_tokens // GRID_W
    row = jnp.repeat(jnp.arange(n_rows, dtype=jnp.float32), GRID_W)
    col = jnp.tile(jnp.arange(GRID_W, dtype=jnp.float32), n_rows)
    axis_dim = rot_dim // 2
    inv_freq = ROPE_THETA ** (-jnp.arange(0, axis_dim, 2, dtype=jnp.float32) / axis_dim)
    ang_r = row[:, None] * inv_freq
    ang_c = col[:, None] * inv_freq
    ang = jnp.concatenate([ang_r, ang_r, ang_c, ang_c], axis=-1)
    return jnp.cos(ang), jnp.sin(ang)


def _rotate_half(z):
    z1, z2 = jnp.split(z, 2, axis=-1)
    return jnp.concatenate([-z2, z1], axis=-1)


def _apply_rope(x, rope):
    cos, sin = rope
    half = x.shape[-1] // 2
    rot = jnp.concatenate([_rotate_half(x[..., :half]), _rotate_half(x[..., half:])], axis=-1)
    return x * cos.astype(x.dtype) + rot * sin.astype(x.dtype)


def _sweep_query_blocks(fn, *qs):
    T = qs[0].shape[-2]
    nb = T // Q_BLOCK

    def split(q):
        q = q.reshape(q.shape[:-2] + (nb, Q_BLOCK, q.shape[-1]))
        return jnp.moveaxis(q, -3, 0)

    out = lax.map(lambda blks: fn(*blks), tuple(split(q) for q in qs))
    out = jnp.moveaxis(out, 0, -3)
    return out.reshape(out.shape[:-3] + (T, out.shape[-1]))


def _attend(q, k, v, scale):
    s = jnp.einsum("bngqd,bnkd->bngqk", q, k).astype(jnp.float32) * scale
    p = jax.nn.softmax(s, axis=-1).astype(v.dtype)
    return jnp.einsum("bngqk,bnkd->bngqd", p, v)


def _centred_shift(p):
    prev = jnp.pad(p[:, :-1], ((0, 0), (1, 0), (0, 0)))
    nxt = jnp.pad(p[:, 1:], ((0, 0), (0, 1), (0, 0)))
    return 0.5 * (prev + nxt)


def _rwkv7_inputs(pr, mu, w0, w_up, a0, a_up, g_up, k_k, k_a):
    pr = pr + (_centred_shift(pr) - pr) * mu
    B, T, _ = pr.shape
    r = pr[..., :RW_W]
    k = pr[..., RW_W:2 * RW_W]
    v = pr[..., 2 * RW_W:3 * RW_W]
    o = 3 * RW_W
    gd = pr[..., o:o + GATE_RANK]
    o += GATE_RANK
    wd = pr[..., o:o + 2 * DECAY_RANK].reshape(B, T, 2, DECAY_RANK)
    o += 2 * DECAY_RANK
    ad = pr[..., o:o + 2 * AAA_RANK].reshape(B, T, 2, AAA_RANK)
    g = jax.nn.sigmoid(gd) @ g_up
    w_log = -jax.nn.softplus(-(w0 + jnp.einsum("btdr,drc->btdc", jnp.tanh(wd), w_up))) - 0.5
    decay = jnp.exp(-jnp.exp(w_log.astype(jnp.float32)))
    a = jax.nn.sigmoid(a0 + jnp.einsum("btdr,drc->btdc", ad, a_up))
    kk = _heads(k * k_k, RW_HEADS).astype(jnp.float32)
    kk = kk / jnp.maximum(jnp.linalg.norm(kk, axis=-1, keepdims=True), 1e-12)
    kk = kk.reshape(B, T, RW_W).astype(k.dtype)
    k_dir = k[:, :, None] * (1.0 + (a - 1.0) * k_a)
    return r, v, g, kk, decay, a, k_dir


def _rwkv7_scan(state0, feats, d, reverse):
    r, v, g, kk, decay, a, k_dir = feats
    B, T, _ = r.shape

    def tm(z):
        return jnp.moveaxis(_heads(z, RW_HEADS).astype(jnp.float32), 1, 0)

    def step(S, inp):
        r_t, w_t, k_t, v_t, kk_t, b_t = inp
        sa = jnp.einsum("bhvk,bhk->bhv", S, kk_t)
        S = S * w_t[..., None, :] - sa[..., :, None] * b_t[..., None, :] + v_t[..., :, None] * k_t[..., None, :]
        return S, jnp.einsum("bhvk,bhk->bhv", S, r_t)

    xs = tuple(tm(z) for z in (r, decay[:, :, d], k_dir[:, :, d], v, kk, kk * a[:, :, d]))
    S, y = lax.scan(step, state0, xs, reverse=reverse)
    return S, jnp.moveaxis(y, 0, 1).reshape(B, T, RW_W).astype(v.dtype)


def _rwkv7_out(y, feats, r_k, ln_w, ln_b):
    r, v, g, kk, decay, a, k_dir = feats
    B, T, _ = y.shape
    yh = _heads(y, RW_HEADS).astype(jnp.float32)
    mean = jnp.mean(yh, axis=-1, keepdims=True)
    var = jnp.mean(jnp.square(yh - mean), axis=-1, keepdims=True)
    yn = ((yh - mean) * lax.rsqrt(var + GN_EPS)).reshape(B, T, RW_W).astype(y.dtype) * ln_w + ln_b
    coef = jnp.sum(_heads(r, RW_HEADS)[:, :, None] * _heads(k_dir, RW_HEADS) * r_k, axis=(2, 4))
    bonus = (coef[..., None] * _heads(v, RW_HEADS)).reshape(B, T, RW_W)
    return (yn + bonus) * g


def _gqa_qkv(pa, q_g, k_g, rope):
    B, T, _ = pa.shape
    nq = GQ_HEADS * GQ_HEAD
    nkv = GQ_KV_HEADS * GQ_HEAD
    q = _rms(pa[..., :nq].reshape(B, T, GQ_HEADS, GQ_HEAD), q_g).transpose(0, 2, 1, 3)
    k = _rms(pa[..., nq:nq + nkv].reshape(B, T, GQ_KV_HEADS, GQ_HEAD), k_g).transpose(0, 2, 1, 3)
    v = pa[..., nq + nkv:].reshape(B, T, GQ_KV_HEADS, GQ_HEAD).transpose(0, 2, 1, 3)
    if rope is not None:
        q = _apply_rope(q, rope)
        k = _apply_rope(k, rope)
    return q.reshape(B, GQ_KV_HEADS, GQ_GROUP, T, GQ_HEAD), k, v


def _even_mixer(a_lat, a_ctx, rope, w_in, w_out, mu, w0, w_up, a0, a_up, g_up, k_k, k_a, r_k,
                ln_w, ln_b, q_g, k_g, need_ctx):
    p_lat = a_lat @ w_in
    p_ctx = a_ctx @ w_in
    rw_args = (mu, w0, w_up, a0, a_up, g_up, k_k, k_a)
    f_ctx = _rwkv7_inputs(p_ctx[..., :RW_IN], *rw_args)
    f_lat = _rwkv7_inputs(p_lat[..., :RW_IN], *rw_args)
    zero = jnp.zeros((a_lat.shape[0], RW_HEADS, RW_HEAD, RW_HEAD), jnp.float32)
    s_fwd, y_cf = _rwkv7_scan(zero, f_ctx, 0, False)
    s_bwd, y_cb = _rwkv7_scan(zero, f_ctx, 1, True)
    _, y_lf = _rwkv7_scan(s_fwd, f_lat, 0, False)
    _, y_lb = _rwkv7_scan(s_bwd, f_lat, 1, True)
    q_l, k_l, v_l = _gqa_qkv(p_lat[..., RW_IN:], q_g, k_g, rope)
    q_c, k_c, v_c = _gqa_qkv(p_ctx[..., RW_IN:], q_g, k_g, None)

    def merge(y_rw, feats, q, k, v):
        rw = _rwkv7_out(y_rw, feats, r_k, ln_w, ln_b)
        at = _sweep_query_blocks(lambda qb: _attend(qb, k, v, GQ_SCALE), q)
        B, N, G, T, dh = at.shape
        at = at.transpose(0, 3, 1, 2, 4).reshape(B, T, N * G * dh)
        return jnp.concatenate([rw, at], axis=-1) @ w_out

    o_lat = merge(y_lf + y_lb, f_lat, q_l, jnp.concatenate([k_c, k_l], axis=2),
                  jnp.concatenate([v_c, v_l], axis=2))
    o_ctx = merge(y_cf + y_cb, f_ctx, q_c, k_c, v_c) if need_ctx else None
    return o_lat, o_ctx


def _mla_qkv(p, q_norm, q_up, kv_norm, kv_up, rope):
    B, T, _ = p.shape
    c_q = _rms(p[..., :Q_LORA], q_norm)
    c_kv = _rms(p[..., Q_LORA:Q_LORA + KV_LORA], kv_norm)
    k_pe = p[..., Q_LORA + KV_LORA:MLA_IN][:, None]
    q = (c_q @ q_up).reshape(B, T, MLA_HEADS, MLA_NOPE + MLA_ROPE).transpose(0, 2, 1, 3)
    kv = (c_kv @ kv_up).reshape(B, T, MLA_HEADS, MLA_NOPE + MLA_V).transpose(0, 2, 1, 3)
    q_nope, q_pe = q[..., :MLA_NOPE], q[..., MLA_NOPE:]
    k_nope, v = kv[..., :MLA_NOPE], kv[..., MLA_NOPE:]
    if rope is not None:
        q_pe = _apply_rope(q_pe, rope)
        k_pe = _apply_rope(k_pe, rope)
    q = jnp.concatenate([q_nope, q_pe], axis=-1)[:, :, None]
    k = jnp.concatenate([k_nope, jnp.broadcast_to(k_pe, (B, MLA_HEADS, T, MLA_ROPE))], axis=-1)
    return q, k, v


def _diff_qkv(p, rope):
    B, T, _ = p.shape
    qk_w = DIFF_HEADS * 2 * DIFF_HEAD
    q = p[..., :qk_w].reshape(B, T, DIFF_HEADS, 2, DIFF_HEAD).transpose(0, 2, 3, 1, 4)
    k = p[..., qk_w:2 * qk_w].reshape(B, T, DIFF_HEADS, 2, DIFF_HEAD).transpose(0, 2, 3, 1, 4)
    v = p[..., 2 * qk_w:].reshape(B, T, DIFF_HEADS, DIFF_V).transpose(0, 2, 1, 3)
    if rope is not None:
        q = _apply_rope(q, rope)
        k = _apply_rope(k, rope)
    return q, k, v


def _diff_attend(q1, q2, k, v, lam):
    s1 = jnp.einsum("bhqd,bhkd->bhqk", q1, k[:, :, 0]).astype(jnp.float32) * DIFF_SCALE
    s2 = jnp.einsum("bhqd,bhkd->bhqk", q2, k[:, :, 1]).astype(jnp.float32) * DIFF_SCALE
    p = jax.nn.softmax(s1, axis=-1) - lam * jax.nn.softmax(s2, axis=-1)
    return jnp.einsum("bhqk,bhkd->bhqd", p.astype(v.dtype), v)


def _odd_mixer(a_lat, a_ctx, rope, w_in, w_out, q_norm, q_up, kv_norm, kv_up, lq1, lk1, lq2, lk2,
               subln, layer_idx, need_ctx):
    p_lat = a_lat @ w_in
    p_ctx = a_ctx @ w_in
    lam_init = 0.8 - 0.6 * math.exp(-0.3 * layer_idx)
    lam = (jnp.exp(jnp.sum(lq1 * lk1).astype(jnp.float32))
           - jnp.exp(jnp.sum(lq2 * lk2).astype(jnp.float32)) + lam_init)
    mla_args = (q_norm, q_up, kv_norm, kv_up)
    mq_l, mk_l, mv_l = _mla_qkv(p_lat[..., :MLA_IN], *mla_args, rope)
    mq_c, mk_c, mv_c = _mla_qkv(p_ctx[..., :MLA_IN], *mla_args, None)
    dq_l, dk_l, dv_l = _diff_qkv(p_lat[..., MLA_IN:], rope)
    dq_c, dk_c, dv_c = _diff_qkv(p_ctx[..., MLA_IN:], None)

    def merge(mq, mk, mv, dq, dk, dv):
        m = _sweep_query_blocks(lambda qb: _attend(qb, mk, mv, MLA_SCALE), mq)
        d = _sweep_query_blocks(lambda a, b: _diff_attend(a, b, dk, dv, lam), dq[:, :, 0], dq[:, :, 1])
        d = _rms(d, subln) * (1.0 - lam_init)
        B, H, _, T, dv_ = m.shape
        m = m[:, :, 0].transpose(0, 2, 1, 3).reshape(B, T, H * dv_)
        d = d.transpose(0, 2, 1, 3).reshape(B, T, DIFF_HEADS * DIFF_V)
        return jnp.concatenate([m, d], axis=-1) @ w_out

    o_lat = merge(mq_l, jnp.concatenate([mk_c, mk_l], axis=2), jnp.concatenate([mv_c, mv_l], axis=2),
                  dq_l, jnp.concatenate([dk_c, dk_l], axis=3), jnp.concatenate([dv_c, dv_l], axis=2))
    o_ctx = merge(mq_c, mk_c, mv_c, dq_c, dk_c, dv_c) if need_ctx else None
    return o_lat, o_ctx


def setup_inputs(seed: int = 0) -> dict:
    key = jax.random.key(seed)
    ks = iter(jax.random.split(key, 64))
    D = D_MODEL

    def nrm(shape, scale):
        return jax.random.normal(next(ks), shape, jnp.float32) * scale

    def gain(shape):
        return 1.0 + nrm(shape, 0.05)

    return {
        "x": nrm((BATCH, SEQ, D), 1.0),
        "c": nrm((BATCH, D), 1.0),
        "ctx": nrm((BATCH, CTX_LEN, D), 1.0),
        "c_ctx": nrm((D,), 1.0),
        "ada_w": nrm((DEPTH, D, N_MOD * D), 0.5 * D ** -0.5),
        "ada_b": nrm((DEPTH, N_MOD * D), 0.01),
        "norm1_g": gain((DEPTH, D)),
        "norm2_g": gain((DEPTH, D)),
        "mlp_w1": nrm((DEPTH, D, MLP_HIDDEN), D ** -0.5),
        "mlp_w2": nrm((DEPTH, MLP_HIDDEN, D), MLP_HIDDEN ** -0.5),
        "final_g": gain((D,)),
        "ev_w_in": nrm((N_EVEN, D, IN_EVEN), D ** -0.5),
        "ev_w_out": nrm((N_EVEN, RW_W + GQ_HEADS * GQ_HEAD, D), D ** -0.5),
        "rw_mu": jax.random.uniform(next(ks), (N_EVEN, RW_IN), jnp.float32),
        "rw_w0": jax.random.uniform(next(ks), (N_EVEN, 2, RW_W), jnp.float32, -2.0, 1.0),
        "rw_w_up": nrm((N_EVEN, 2, DECAY_RANK, RW_W), DECAY_RANK ** -0.5),
        "rw_a0": nrm((N_EVEN, 2, RW_W), 0.5),
        "rw_a_up": nrm((N_EVEN, 2, AAA_RANK, RW_W), AAA_RANK ** -0.5),
        "rw_g_up": nrm((N_EVEN, GATE_RANK, RW_W), GATE_RANK ** -0.5),
        "rw_k_k": 0.85 + nrm((N_EVEN, RW_W), 0.05),
        "rw_k_a": gain((N_EVEN, RW_W)),
        "rw_r_k": nrm((N_EVEN, RW_HEADS, RW_HEAD), 0.1),
        "rw_ln_w": gain((N_EVEN, RW_W)),
        "rw_ln_b": nrm((N_EVEN, RW_W), 0.01),
        "gq_q_norm": gain((N_EVEN, GQ_HEAD)),
        "gq_k_norm": gain((N_EVEN, GQ_HEAD)),
        "od_w_in": nrm((N_ODD, D, IN_ODD), D ** -0.5),
        "od_w_out": nrm((N_ODD, MLA_HEADS * MLA_V + DIFF_HEADS * DIFF_V, D), D ** -0.5),
        "mla_q_norm": gain((N_ODD, Q_LORA)),
        "mla_q_up": nrm((N_ODD, Q_LORA, MLA_HEADS * (MLA_NOPE + MLA_ROPE)), Q_LORA ** -0.5),
        "mla_kv_norm": gain((N_ODD, KV_LORA)),
        "mla_kv_up": nrm((N_ODD, KV_LORA, MLA_HEADS * (MLA_NOPE + MLA_V)), KV_LORA ** -0.5),
        "diff_lq1": nrm((N_ODD, DIFF_HEAD), 0.1),
        "diff_lk1": nrm((N_ODD, DIFF_HEAD), 0.1),
        "diff_lq2": nrm((N_ODD, DIFF_HEAD), 0.1),
        "diff_lk2": nrm((N_ODD, DIFF_HEAD), 0.1),
        "diff_subln": gain((N_ODD, DIFF_V)),
    }


def reference(x, c, ctx, c_ctx, ada_w, ada_b, norm1_g, norm2_g, mlp_w1, mlp_w2, final_g,
              ev_w_in, ev_w_out, rw_mu, rw_w0, rw_w_up, rw_a0, rw_a_up, rw_g_up, rw_k_k, rw_k_a,
              rw_r_k, rw_ln_w, rw_ln_b, gq_q_norm, gq_k_norm,
              od_w_in, od_w_out, mla_q_norm, mla_q_up, mla_kv_norm, mla_kv_up,
              diff_lq1, diff_lk1, diff_lq2, diff_lk2, diff_subln):
    n_lat = x.shape[1]
    rope_gq = _axial_rope_tables(n_lat, GQ_HEAD)
    rope_64 = _axial_rope_tables(n_lat, MLA_ROPE)
    h = x
    s = ctx
    for i in range(DEPTH):
        last = i == DEPTH - 1
        mod = (jax.nn.silu(c) @ ada_w[i] + ada_b[i])[:, None, :]
        mod_c = jax.nn.silu(c_ctx) @ ada_w[i] + ada_b[i]
        sh1, sc1, g1, sh2, sc2, g2 = jnp.split(mod, N_MOD, axis=-1)
        csh1, csc1, cg1, csh2, csc2, cg2 = jnp.split(mod_c, N_MOD, axis=-1)
        a_lat = _rms(h, norm1_g[i]) * (1.0 + sc1) + sh1
        a_ctx = _rms(s, norm1_g[i]) * (1.0 + csc1) + csh1
        j = i // 2
        if i % 2 == 0:
            o_lat, o_ctx = _even_mixer(a_lat, a_ctx, rope_gq, ev_w_in[j], ev_w_out[j], rw_mu[j], rw_w0[j],
                                       rw_w_up[j], rw_a0[j], rw_a_up[j], rw_g_up[j], rw_k_k[j], rw_k_a[j],
                                       rw_r_k[j], rw_ln_w[j], rw_ln_b[j], gq_q_norm[j], gq_k_norm[j],
                                       not last)
        else:
            o_lat, o_ctx = _odd_mixer(a_lat, a_ctx, rope_64, od_w_in[j], od_w_out[j], mla_q_norm[j],
                                      mla_q_up[j], mla_kv_norm[j], mla_kv_up[j], diff_lq1[j], diff_lk1[j],
                                      diff_lq2[j], diff_lk2[j], diff_subln[j], i, not last)
        h = h + g1 * o_lat
        h = h + g2 * _mlp(_rms(h, norm2_g[i]) * (1.0 + sc2) + sh2, mlp_w1[i], mlp_w2[i])
        if not last:
            s = s + cg1 * o_ctx
            s = s + cg2 * _mlp(_rms(s, norm2_g[i]) * (1.0 + csc2) + csh2, mlp_w1[i], mlp_w2[i])
    return _rms(h, final_g)
```

```python
import math
from contextlib import ExitStack
import numpy as np
import concourse.bass as bass
import concourse.mybir as mybir
from concourse.bass_utils import run_bass_kernel_spmd

F32 = mybir.dt.float32
BF16 = mybir.dt.bfloat16
AF = mybir.ActivationFunctionType
ALU = mybir.AluOpType
AX = mybir.AxisListType

D = 4096
KC = D // 128
CTX = 256
HID = 4 * D
EPS = 1e-6
GN_EPS = 64e-5
RW_IN = 6784
IN_EVEN = 9856
IN_ODD = 7488
ENGS = ("pe", "act", "dve", "pool", "sp")
NRING = 8


class Buf:
    __slots__ = ("w", "r")

    def __init__(self):
        self.w = None
        self.r = {}


class Prog:
    def __init__(self, nc, sems, rings):
        self.nc = nc
        self.eng = {"pe": nc.tensor, "act": nc.scalar, "dve": nc.vector, "pool": nc.gpsimd, "sp": nc.sync}
        self.cnt = {e: 0 for e in ENGS}
        self.sem = sems
        self.ring = rings
        self.dma_i = {e: 0 for e in ENGS}
        self.waited = {e: {} for e in ENGS}
        self.bufs = {}
        self.out_toks = []
        self.all_dma = {}

    def buf(self, key):
        b = self.bufs.get(key)
        if b is None:
            b = self.bufs[key] = Buf()
        return b

    def _wait(self, e, tok):
        s, v = tok
        w = self.waited[e]
        if w.get(s, 0) >= v:
            return
        w[s] = v
        self.eng[e].wait_ge(s, v)

    def _sync(self, e, reads, writes):
        own = self.sem[e]
        pe = e == "pe"
        for b in reads:
            if b.w is not None and not (pe and b.w[0] is own):
                self._wait(e, b.w)
        for b in writes:
            if b.w is not None and not (pe and b.w[0] is own):
                self._wait(e, b.w)
            for s, v in b.r.items():
                if not (pe and s is own):
                    self._wait(e, (s, v))

    def _mark(self, tok, reads, writes):
        s, v = tok
        for b in reads:
            if b.r.get(s, 0) < v:
                b.r[s] = v
        for b in writes:
            b.w = tok
            b.r = {}

    def op(self, e, fn, reads=(), writes=()):
        self._sync(e, reads, writes)
        self.cnt[e] += 1
        sem = self.sem[e]
        fn(self.eng[e]).then_inc(sem, 1)
        tok = (sem, self.cnt[e])
        self._mark(tok, reads, writes)
        return tok

    def dma(self, e, out, in_, reads=(), writes=(), is_output=False):
        self._sync(e, reads, writes)
        i = self.dma_i[e]
        self.dma_i[e] += 1
        sem = self.ring[e][i % NRING]
        val = 16 * (i // NRING + 1)
        if val > 16:
            self._wait(e, (sem, val - 16))
        self.eng[e].dma_start(out=out, in_=in_).then_inc(sem, 16)
        tok = (sem, val)
        self.all_dma[sem] = val
        self._mark(tok, reads, writes)
        if is_output:
            self.out_toks.append(tok)
        return tok

    def barrier(self):
        for e in ENGS:
            for e2 in ENGS:
                if e2 != e and self.cnt[e2] > 0:
                    self._wait(e, (self.sem[e2], self.cnt[e2]))
            for s, v in self.all_dma.items():
                self._wait(e, (s, v))

    def finish(self):
        for tok in self.out_toks:
            self._wait("sp", tok)


class Tl:
    __slots__ = ("t", "b")

    def __init__(self, t):
        self.t = t
        self.b = Buf()


def rope_tables(seq, rot_dim):
    n_rows = seq // 64
    row = np.repeat(np.arange(n_rows, dtype=np.float32), 64)
    col = np.tile(np.arange(64, dtype=np.float32), n_rows)
    axis_dim = rot_dim // 2
    inv_freq = (np.float32(10000.0) ** (-np.arange(0, axis_dim, 2, dtype=np.float32) / np.float32(axis_dim))).astype(np.float32)
    ang_r = row[:, None] * inv_freq
    ang_c = col[:, None] * inv_freq
    ang = np.concatenate([ang_r, ang_r, ang_c, ang_c], axis=-1).astype(np.float32)
    return np.cos(ang).astype(np.float32), np.sin(ang).astype(np.float32)


def rot_matrix(rot_dim, reps):
    R = np.zeros((128, 128), np.float32)
    half = rot_dim // 2
    q = half // 2
    for rp in range(reps):
        for hb in range(2):
            o = rp * rot_dim + hb * half
            for i in range(q):
                R[o + q + i, o + i] = -1.0
                R[o + i, o + q + i] = 1.0
    return R


def split_tiles(seq, nmax):
    tiles = [(0, CTX, True)]
    nt = -(-seq // nmax)
    base = -(-seq // nt)
    base = -(-base // 8) * 8
    c = 0
    while c < seq:
        n = min(base, seq - c)
        tiles.append((CTX + c, n, False))
        c += n
    return tiles


def build(SEQ, dbg=False):
    T = CTX + SEQ
    nc = bass.Bass("TRN2", target_bir_lowering=False)

    def din(name, shape):
        return nc.dram_tensor(name, list(shape), F32, kind="ExternalInput").ap()

    def dscr(name, shape, dt=F32):
        return nc.dram_tensor(name, list(shape), dt, kind="ExternalOutput" if dbg else "Internal").ap()

    hT0 = din("hT0", [KC, 128, T])
    cT = din("cT", [128, KC, 2])
    consts = din("consts", [128, 4 * 128 + 2])
    ropes = din("ropes", [4, 128, T])
    ada_w = din("ada_w", [2, 48, 128, KC, 512])
    ada_bT = din("ada_bT", [128, 2, 192])
    n1g = din("n1g", [128, 2, KC])
    n2g = din("n2g", [128, 2, KC])
    fing = din("fing", [128, KC])
    mlp_w1_f = din("mlp_w1", [2, 128, 128, KC, 128])
    mlp_w2_f = din("mlp_w2", [2, KC, 128, 128, 128])
    ev_w_in_f = din("ev_w_in", [78, 128, KC, 128])
    ev_w_out_f = din("ev_w_out", [KC, 128, KC, 128])
    rw_mu = din("rw_mu", [128, 57])
    rw_w0 = din("rw_w0", [128, 2, 16])
    rw_a0 = din("rw_a0", [128, 2, 16])
    rw_w_up = din("rw_w_up", [96, 2, 2048])
    rw_a_up = din("rw_a_up", [96, 2, 2048])
    rw_g_up = din("rw_g_up", [128, 2, 2048])
    rw_vec = din("rw_vec", [128, 5, 16])
    gq_g = din("gq_g", [128, 2])
    od_w_in_f = din("od_w_in", [59, 128, KC, 128])
    od_w_out_f = din("od_w_out", [KC, 128, KC, 128])
    mla_qn = din("mla_qn", [128, 6])
    mla_kvn = din("mla_kvn", [128, 4])
    mla_q_up_f = din("mla_q_up", [16, 128, 6, 192])
    mla_kv_up_f = din("mla_kv_up", [16, 128, 4, 256])
    dlam = din("dlam", [1, 4, 64])
    subln = din("subln", [128, 1])
    outT = nc.dram_tensor("outT", [KC, 128, SEQ], F32, kind="ExternalOutput").ap()

    def bfw(name, src):
        return nc.dram_tensor(name, list(src.shape), BF16, kind="Internal").ap()

    mlp_w1 = bfw("mlp_w1_b", mlp_w1_f)
    mlp_w2 = bfw("mlp_w2_b", mlp_w2_f)
    ev_w_in = bfw("ev_w_in_b", ev_w_in_f)
    ev_w_out = bfw("ev_w_out_b", ev_w_out_f)
    od_w_in = bfw("od_w_in_b", od_w_in_f)
    od_w_out = bfw("od_w_out_b", od_w_out_f)
    mla_q_up = bfw("mla_q_up_b", mla_q_up_f)
    mla_kv_up = bfw("mla_kv_up_b", mla_kv_up_f)
    CASTS = [(ev_w_in, ev_w_in_f), (ev_w_out, ev_w_out_f), (mlp_w1, mlp_w1_f), (mlp_w2, mlp_w2_f),
             (od_w_in, od_w_in_f), (mla_q_up, mla_q_up_f), (mla_kv_up, mla_kv_up_f), (od_w_out, od_w_out_f)]

    hT1 = dscr("hT1", [KC, 128, T])
    sc_r = dscr("sc_r", [16, 128, T])
    sc_kk = dscr("sc_kk", [16, 128, T])
    sc_kd = [dscr(f"sc_kd{d}", [16, 128, T]) for d in range(2)]
    sc_b = [dscr(f"sc_b{d}", [16, 128, T]) for d in range(2)]
    sc_w = [dscr(f"sc_w{d}", [16, 128, T]) for d in range(2)]
    gT = dscr("gT", [16, 128, T])
    bonT = dscr("bonT", [16, 128, T])
    v_sc = dscr("v_sc", [2, T, 16, 64])
    y_sc = [dscr(f"y_sc{d}", [T, 2, 16, 64]) for d in range(2)]
    qT = dscr("qT", [16, 128, T], BF16)
    kT = dscr("kT", [4, 128, T], BF16)
    vg_tm = dscr("vg_tm", [T, 512], BF16)
    attT = dscr("attT", [16, 128, T], BF16)
    mqn = dscr("mqn", [16, 128, T], BF16)
    mqp = dscr("mqp", [16, 64, T], BF16)
    mkn = dscr("mkn", [16, 128, T], BF16)
    mkp = dscr("mkp", [64, T], BF16)
    mv_tm = dscr("mv_tm", [T, 2048], BF16)
    dqT = dscr("dqT", [16, 128, T], BF16)
    dkT = dscr("dkT", [16, 128, T], BF16)
    dv_tm = dscr("dv_tm", [T, 2048], BF16)
    att1 = dscr("att1", [32, 128, T], BF16)

    es = ExitStack()
    with es:
        sems = {e: es.enter_context(nc.semaphore("s_" + e)) for e in ENGS}
        rings = {e: [es.enter_context(nc.semaphore(f"r_{e}{i}")) for i in range(NRING)] for e in ENGS}
        P = Prog(nc, sems, rings)

        def sb(st, name, shape, dt=F32):
            return Tl(st.enter_context(nc.sbuf_tensor(name, list(shape), dt)))

        def ps(st, name, shape):
            return Tl(st.enter_context(nc.psum_tensor(name, list(shape), F32)))

        for dstw, srcw in CASTS:
            tot = 1
            for d_ in srcw.shape:
                tot *= d_
            names = " ".join(f"a{i}" for i in range(len(srcw.shape)))
            fs = srcw.rearrange(f"{names} -> ({names})").rearrange("(x y) -> x y", y=2048)
            fd = dstw.rearrange(f"{names} -> ({names})").rearrange("(x y) -> x y", y=2048)
            rows = tot // 2048
            step = 4096
            r0 = 0
            while r0 < rows:
                r1 = min(rows, r0 + step)
                P.dma("pool", fd[r0:r1, :], fs[r0:r1, :], writes=[Buf()])
                r0 = r1

        cst = sb(es, "cst", [128, 4 * 128 + 2])
        ones = sb(es, "ones", [128, 128])
        ones16 = sb(es, "ones16", [128, 128], BF16)
        epsc = sb(es, "epsc", [128, 3])
        modT = sb(es, "modT", [128, 2, 192, 2])
        vecs = sb(es, "vecs", [128, 2, 2, KC])
        fg = sb(es, "fg", [128, KC])
        gsc = sb(es, "gsc", [128, 2, 2, 2, KC])

        P.dma("sp", cst.t[:], consts[:, :], writes=[cst.b])
        P.op("dve", lambda e: e.memset(ones.t[:], 1.0), writes=[ones.b])
        P.op("dve", lambda e: e.memset(ones16.t[:], 1.0), writes=[ones16.b])
        P.op("dve", lambda e: e.memset(epsc.t[:, 0:1], EPS), writes=[epsc.b])
        P.op("dve", lambda e: e.memset(epsc.t[:, 1:2], GN_EPS), writes=[epsc.b])
        P.op("dve", lambda e: e.memset(epsc.t[:, 2:3], 1.0), writes=[epsc.b])
        P.dma("sp", vecs.t[:, :, 0, :], n1g[:, :, :], writes=[vecs.b])
        P.dma("sp", vecs.t[:, :, 1, :], n2g[:, :, :], writes=[vecs.b])
        P.dma("sp", fg.t[:], fing[:, :], writes=[fg.b])
        ident = cst.t[:, 0:128]
        blk64 = cst.t[:, 128:256]
        R128 = cst.t[:, 256:384]
        R64 = cst.t[:, 384:512]
        sel2 = cst.t[:, 512:514]

        with ExitStack() as ph:
            sc = sb(ph, "p0_c", [128, KC, 2])
            ab = sb(ph, "p0_ab", [128, 2, 192])
            slabs = [sb(ph, f"p0_w{i}", [128, KC, 512]) for i in range(2)]
            pm = [ps(ph, f"p0_ps{i}", [128, 8]) for i in range(2)]
            P.dma("sp", sc.t[:], cT[:, :, :], writes=[sc.b])
            P.dma("sp", ab.t[:], ada_bT[:, :, :], writes=[ab.b])
            P.op("act", lambda e: e.activation(out=sc.t[:], in_=sc.t[:], func=AF.Silu), reads=[sc.b], writes=[sc.b])
            i = 0
            for l in range(2):
                for nt in range(48):
                    sl = slabs[i % 2]
                    pp = pm[i % 2]
                    i += 1
                    P.dma("sp", sl.t[:], ada_w[l, nt], writes=[sl.b])
                    for j in range(4):
                        for kc in range(KC):
                            P.op("pe", lambda e: e.matmul(pp.t[:, 2 * j:2 * j + 2], sl.t[:, kc, j * 128:(j + 1) * 128],
                                                          sc.t[:, kc, :], start=(kc == 0), stop=(kc == KC - 1)),
                                 reads=[sl.b, sc.b], writes=[pp.b])
                    P.op("dve", lambda e: e.tensor_tensor(
                        out=modT.t[:, l, nt * 4:(nt + 1) * 4, :], in0=pp.t[:, 0:8].rearrange("p (j r) -> p j r", r=2),
                        in1=ab.t[:, l, nt * 4:(nt + 1) * 4].unsqueeze(2).broadcast_to([128, 4, 2]), op=ALU.add),
                        reads=[pp.b, ab.b], writes=[modT.b])
            for l in range(2):
                for w in range(2):
                    for r in range(2):
                        s0 = (1 + 3 * w) * 32
                        P.op("dve", lambda e: e.scalar_tensor_tensor(
                            out=gsc.t[:, l, w, r, :], in0=modT.t[:, l, s0:s0 + 32, r], scalar=1.0,
                            in1=vecs.t[:, l, w, :], op0=ALU.add, op1=ALU.mult),
                            reads=[modT.b, vecs.b], writes=[gsc.b])
        P.barrier()

        def mod(l, sec, r, c):
            return modT.t[:, l, sec * 32 + c, r:r + 1]

        def rms_mod(src, W, l, which, r, ph_tiles, dst):
            sq, pss, rstd = ph_tiles
            for kc in range(KC):
                s = sq[kc % 2]
                P.op("act", lambda e: e.activation(out=s.t[:, :W], in_=src.t[:, kc, :W], func=AF.Square),
                     reads=[src.b], writes=[s.b])
                P.op("pe", lambda e: e.matmul(pss.t[:, :W], ones.t[:], s.t[:, :W], start=(kc == 0),
                                              stop=(kc == KC - 1)), reads=[s.b, ones.b], writes=[pss.b])
            P.op("act", lambda e: e.activation(out=rstd.t[:, :W], in_=pss.t[:, :W], func=AF.Sqrt, scale=1.0 / D,
                                               bias=epsc.t[:, 0:1]), reads=[pss.b, epsc.b], writes=[rstd.b])
            P.op("dve", lambda e: e.reciprocal(out=rstd.t[:, :W], in_=rstd.t[:, :W]), reads=[rstd.b], writes=[rstd.b])
            for kc in range(KC):
                s = sq[kc % 2]
                P.op("dve", lambda e: e.tensor_tensor(out=s.t[:, :W], in0=src.t[:, kc, :W], in1=rstd.t[:, :W],
                                                      op=ALU.mult), reads=[src.b, rstd.b], writes=[s.b])
                P.op("act", lambda e: e.activation(out=dst.t[:, kc, :W], in_=s.t[:, :W], func=AF.Identity,
                                                   scale=gsc.t[:, l, which, r, kc:kc + 1], bias=mod(l, 3 * which, r, kc)),
                     reads=[s.b, gsc.b, modT.b], writes=[dst.b])

        def transpose_store(src_ap, n, dsts_fn, tp, tsb, cnt, rd):
            t0 = 0
            while t0 < n:
                m = min(128, n - t0)
                pt = tp[cnt[0] % 2]
                ts_ = tsb[cnt[0] % 2]
                cnt[0] += 1
                P.op("pe", lambda e: e.transpose(pt.t[:m, :], src_ap[:, t0:t0 + m], ident), reads=rd + [cst.b],
                     writes=[pt.b])
                P.op("act", lambda e: e.activation(out=ts_.t[:m, :], in_=pt.t[:m, :], func=AF.Copy), reads=[pt.b],
                     writes=[ts_.b])
                for (dap, lo_, hi_) in dsts_fn(t0, m):
                    P.dma("sp", dap, ts_.t[:m, lo_:hi_], reads=[ts_.b])
                t0 += m

        tiles1 = split_tiles(SEQ, 448)
        with ExitStack() as ph:
            WMAX = max(n for _, n, _ in tiles1) + 2
            hs = sb(ph, "p1_hs", [128, KC, WMAX])
            hsb = sb(ph, "p1_hsb", [128, KC, WMAX], BF16)
            sq = [sb(ph, f"p1_sq{i}", [128, WMAX]) for i in range(2)]
            rstd = sb(ph, "p1_rstd", [128, WMAX])
            slab = [sb(ph, f"p1_slab{i}", [128, KC, 128], BF16) for i in range(3)]
            aup = sb(ph, "p1_aup", [96, 2, 2048])
            wup = sb(ph, "p1_wup", [96, 2, 2048])
            gup = sb(ph, "p1_gup", [128, 2, 2048])
            mu = sb(ph, "p1_mu", [128, 57])
            omu = sb(ph, "p1_omu", [128, 57])
            hmu = sb(ph, "p1_hmu", [128, 57])
            w0s = sb(ph, "p1_w0", [128, 2, 16])
            a0s = sb(ph, "p1_a0", [128, 2, 16])
            rv = sb(ph, "p1_rv", [128, 5, 16])
            omka = sb(ph, "p1_omka", [128, 16])
            gqg = sb(ph, "p1_gqg", [128, 2])
            rope = sb(ph, "p1_rope", [128, 2, WMAX])
            psb = sb(ph, "p1_psb", [128, WMAX])
            names = ["t1", "prr", "prk", "prv", "ad0", "ad1", "wd0", "wd1", "sg0", "sg1", "kx", "t2", "t3", "a_0",
                     "a_1", "kk", "o0", "o1", "o2", "o3"]
            tt = {nm: sb(ph, "p1_" + nm, [128, WMAX]) for nm in names}
            pp = [ps(ph, f"p1_pp{i}", [128, 512]) for i in range(2)]
            pq = [ps(ph, f"p1_pq{i}", [128, 512]) for i in range(2)]
            pss = ps(ph, "p1_pss", [128, 512])
            tp = [ps(ph, f"p1_tp{i}", [128, 128]) for i in range(2)]
            tsb = [sb(ph, f"p1_ts{i}", [128, 128]) for i in range(2)]
            tsb16 = [sb(ph, f"p1_tsh{i}", [128, 128], BF16) for i in range(2)]
            ob16 = [sb(ph, f"p1_ob{i}", [128, WMAX], BF16) for i in range(2)]
            obi = [0]
            for dst, src in ((aup, rw_a_up), (wup, rw_w_up), (gup, rw_g_up)):
                P.dma("sp", dst.t[:], src[:, :, :], writes=[dst.b])
            P.dma("sp", mu.t[:], rw_mu[:, :], writes=[mu.b])
            P.dma("sp", w0s.t[:], rw_w0[:, :, :], writes=[w0s.b])
            P.dma("sp", a0s.t[:], rw_a0[:, :, :], writes=[a0s.b])
            P.dma("sp", rv.t[:], rw_vec[:, :, :], writes=[rv.b])
            P.dma("sp", gqg.t[:], gq_g[:, :], writes=[gqg.b])
            P.op("dve", lambda e: e.tensor_scalar(out=omu.t[:], in0=mu.t[:], scalar1=-1.0, scalar2=1.0, op0=ALU.mult,
                                                  op1=ALU.add), reads=[mu.b], writes=[omu.b])
            P.op("dve", lambda e: e.tensor_scalar(out=hmu.t[:], in0=mu.t[:], scalar1=0.5, scalar2=None, op0=ALU.mult),
                 reads=[mu.b], writes=[hmu.b])
            P.op("dve", lambda e: e.tensor_scalar(out=omka.t[:], in0=rv.t[:, 1, :], scalar1=-1.0, scalar2=1.0,
                                                  op0=ALU.mult, op1=ALU.add), reads=[rv.b], writes=[omka.b])
            cnt = [0]
            gi = [0]

            def proj(W_dram, g, M, Wd):
                sl = slab[gi[0] % 3]
                p_ = pp[gi[0] % 2]
                gi[0] += 1
                P.dma("sp", sl.t[:], W_dram[g], writes=[sl.b])
                for kc in range(KC):
                    P.op("pe", lambda e: e.matmul(p_.t[:M, :Wd], sl.t[:, kc, :M], hsb.t[:, kc, :Wd], start=(kc == 0),
                                                  stop=(kc == KC - 1)), reads=[sl.b, hsb.b], writes=[p_.b])
                return p_

            for (c0, n, is_ctx) in tiles1:
                Wd = n + 2
                r = 1 if is_ctx else 0
                s_lo = 0 if is_ctx else CTX
                s_hi = CTX if is_ctx else T
                lo = max(c0 - 1, s_lo)
                hi = min(c0 + n + 1, s_hi)
                left_ok = lo == c0 - 1
                right_ok = hi == c0 + n + 1
                o0_ = lo - (c0 - 1)
                if not left_ok:
                    P.op("dve", lambda e: e.memset(hs.t[:, :, 0:1], 0.0), writes=[hs.b])
                if not right_ok:
                    P.op("dve", lambda e, Wd=Wd: e.memset(hs.t[:, :, Wd - 1:Wd], 0.0), writes=[hs.b])
                for half in range(2):
                    P.dma("sp", hs.t[:, half * 16:(half + 1) * 16, o0_:o0_ + hi - lo],
                          hT0[half * 16:(half + 1) * 16, :, lo:hi].rearrange("c p t -> p c t"), writes=[hs.b])
                if not is_ctx:
                    P.dma("sp", rope.t[:, :, :n], ropes[0:2, :, c0:c0 + n].rearrange("j p t -> p j t"), writes=[rope.b])
                rms_mod(hs, Wd, 0, 0, r, (sq, pss, rstd), hsb)

                def mixed(g, M, mc, dst):
                    p_ = proj(ev_w_in, g, M, Wd)
                    P.op("act", lambda e: e.activation(out=psb.t[:M, :Wd], in_=p_.t[:M, :Wd], func=AF.Copy),
                         reads=[p_.b], writes=[psb.b])
                    if not left_ok:
                        P.op("dve", lambda e: e.memset(psb.t[:M, 0:1], 0.0), writes=[psb.b])
                    if not right_ok:
                        P.op("dve", lambda e: e.memset(psb.t[:M, Wd - 1:Wd], 0.0), writes=[psb.b])
                    t1 = tt["t1"]
                    P.op("dve", lambda e: e.tensor_tensor(out=t1.t[:M, :n], in0=psb.t[:M, 0:n], in1=psb.t[:M, 2:n + 2],
                                                          op=ALU.add), reads=[psb.b], writes=[t1.b])
                    P.op("dve", lambda e: e.tensor_scalar(out=t1.t[:M, :n], in0=t1.t[:M, :n],
                                                          scalar1=hmu.t[:M, mc:mc + 1], scalar2=None, op0=ALU.mult),
                         reads=[t1.b, hmu.b], writes=[t1.b])
                    P.op("dve", lambda e: e.scalar_tensor_tensor(out=dst.t[:M, :n], in0=psb.t[:M, 1:n + 1],
                                                                 scalar=omu.t[:M, mc:mc + 1], in1=t1.t[:M, :n],
                                                                 op0=ALU.mult, op1=ALU.add),
                         reads=[psb.b, t1.b, omu.b], writes=[dst.b])

                def T_(nm):
                    return tt[nm]

                def ew(e_, f, rd, wr):
                    P.op(e_, f, reads=[x.b for x in rd], writes=[x.b for x in wr])

                def store(dst_ap, src, M=128):
                    P.dma("sp", dst_ap, src.t[:M, :n], reads=[src.b])

                for j, nm in enumerate(("wd0", "wd1", "ad0", "ad1")):
                    mixed(50 + j, 96, 53 + j, tt[nm])
                for nm in ("wd0", "wd1"):
                    x_ = tt[nm]
                    ew("act", lambda e: e.activation(out=x_.t[:96, :n], in_=x_.t[:96, :n], func=AF.Tanh), [x_], [x_])
                for j in range(2):
                    x_ = tt[f"sg{j}"]
                    mixed(48 + j, 128, 48 + j, x_)
                    ew("act", lambda e: e.activation(out=x_.t[:, :n], in_=x_.t[:, :n], func=AF.Sigmoid), [x_], [x_])
                prr, prk, prv, kx, t2, t3, kk = (tt[k_] for k_ in ("prr", "prk", "prv", "kx", "t2", "t3", "kk"))
                for c in range(16):
                    cs = slice(c * 128, (c + 1) * 128)
                    mixed(c, 128, c, prr)
                    mixed(16 + c, 128, 16 + c, prk)
                    mixed(32 + c, 128, 32 + c, prv)
                    store(sc_r[c, :, c0:c0 + n], prr)
                    a_ = [tt["a_0"], tt["a_1"]]
                    for d in range(2):
                        q_ = pq[d]
                        adt = tt[f"ad{d}"]
                        P.op("pe", lambda e: e.matmul(q_.t[:, :n], aup.t[:96, d, cs], adt.t[:96, :n], start=True,
                                                      stop=True), reads=[aup.b, adt.b], writes=[q_.b])
                        ew("act", lambda e: e.activation(out=a_[d].t[:, :n], in_=q_.t[:, :n], func=AF.Sigmoid,
                                                         bias=a0s.t[:, d, c:c + 1]), [q_, a0s], [a_[d]])
                    ew("dve", lambda e: e.tensor_scalar(out=kx.t[:, :n], in0=prk.t[:, :n], scalar1=rv.t[:, 0, c:c + 1],
                                                        scalar2=None, op0=ALU.mult), [prk, rv], [kx])
                    ew("act", lambda e: e.activation(out=t2.t[:, :n], in_=kx.t[:, :n], func=AF.Square), [kx], [t2])
                    q_ = pq[0]
                    P.op("pe", lambda e: e.matmul(q_.t[:, :n], blk64, t2.t[:, :n], start=True, stop=True),
                         reads=[cst.b, t2.b], writes=[q_.b])
                    ew("act", lambda e: e.activation(out=t3.t[:, :n], in_=q_.t[:, :n], func=AF.Sqrt), [q_], [t3])
                    ew("dve", lambda e: e.tensor_scalar(out=t3.t[:, :n], in0=t3.t[:, :n], scalar1=1e-12, scalar2=None,
                                                        op0=ALU.max), [t3], [t3])
                    ew("dve", lambda e: e.reciprocal(out=t3.t[:, :n], in_=t3.t[:, :n]), [t3], [t3])
                    ew("dve", lambda e: e.tensor_tensor(out=kk.t[:, :n], in0=kx.t[:, :n], in1=t3.t[:, :n], op=ALU.mult),
                       [kx, t3], [kk])
                    store(sc_kk[c, :, c0:c0 + n], kk)
                    for d in range(2):
                        od, ob = tt[f"o{d}"], tt[f"o{2 + d}"]
                        ew("dve", lambda e: e.tensor_scalar(out=t2.t[:, :n], in0=a_[d].t[:, :n],
                                                            scalar1=rv.t[:, 1, c:c + 1], scalar2=omka.t[:, c:c + 1],
                                                            op0=ALU.mult, op1=ALU.add), [a_[d], rv, omka], [t2])
                        ew("dve", lambda e: e.tensor_tensor(out=od.t[:, :n], in0=prk.t[:, :n], in1=t2.t[:, :n],
                                                            op=ALU.mult), [prk, t2], [od])
                        store(sc_kd[d][c, :, c0:c0 + n], od)
                        ew("dve", lambda e: e.tensor_tensor(out=ob.t[:, :n], in0=kk.t[:, :n], in1=a_[d].t[:, :n],
                                                            op=ALU.mult), [kk, a_[d]], [ob])
                        store(sc_b[d][c, :, c0:c0 + n], ob)
                    ew("dve", lambda e: e.tensor_tensor(out=t2.t[:, :n], in0=tt["o0"].t[:, :n], in1=tt["o1"].t[:, :n],
                                                        op=ALU.add), [tt["o0"], tt["o1"]], [t2])
                    ew("dve", lambda e: e.tensor_tensor(out=t2.t[:, :n], in0=t2.t[:, :n], in1=prr.t[:, :n], op=ALU.mult),
                       [t2, prr], [t2])
                    ew("dve", lambda e: e.tensor_scalar(out=t2.t[:, :n], in0=t2.t[:, :n], scalar1=rv.t[:, 2, c:c + 1],
                                                        scalar2=None, op0=ALU.mult), [t2, rv], [t2])
                    q_ = pq[1]
                    P.op("pe", lambda e: e.matmul(q_.t[:, :n], blk64, t2.t[:, :n], start=True, stop=True),
                         reads=[cst.b, t2.b], writes=[q_.b])
                    ew("dve", lambda e: e.tensor_tensor(out=t3.t[:, :n], in0=q_.t[:, :n], in1=prv.t[:, :n], op=ALU.mult),
                       [q_, prv], [t3])
                    store(bonT[c, :, c0:c0 + n], t3)
                    for d in range(2):
                        q_ = pq[d]
                        wdt = tt[f"wd{d}"]
                        P.op("pe", lambda e: e.matmul(q_.t[:, :n], wup.t[:96, d, cs], wdt.t[:96, :n], start=True,
                                                      stop=True), reads=[wup.b, wdt.b], writes=[q_.b])
                        ew("act", lambda e: e.activation(out=t2.t[:, :n], in_=q_.t[:, :n], func=AF.Sigmoid,
                                                         bias=w0s.t[:, d, c:c + 1]), [q_, w0s], [t2])
                        ow = tt[f"o{d}"]
                        ew("act", lambda e: e.activation(out=ow.t[:, :n], in_=t2.t[:, :n], func=AF.Exp,
                                                         scale=-math.exp(-0.5)), [t2], [ow])
                        store(sc_w[d][c, :, c0:c0 + n], ow)
                    q_ = pq[0]
                    for j in range(2):
                        sgj = tt[f"sg{j}"]
                        P.op("pe", lambda e: e.matmul(q_.t[:, :n], gup.t[:, j, cs], sgj.t[:, :n], start=(j == 0),
                                                      stop=(j == 1)), reads=[gup.b, sgj.b], writes=[q_.b])
                    og = tt["o2"]
                    ew("act", lambda e: e.activation(out=og.t[:, :n], in_=q_.t[:, :n], func=AF.Copy), [q_], [og])
                    store(gT[c, :, c0:c0 + n], og)
                    transpose_store(prv.t, n, lambda t0, m: [(v_sc[l_, c0 + t0:c0 + t0 + m, c, :], l_ * 64, l_ * 64 + 64) for l_ in range(2)],
                                    tp, tsb, cnt, [prv.b])

                def qk_head(g, gcol, dst_ap):
                    p_ = proj(ev_w_in, g, 128, Wd)
                    o16 = ob16[obi[0] % 2]
                    obi[0] += 1
                    ew("act", lambda e: e.activation(out=t2.t[:, :n], in_=p_.t[:, 1:n + 1], func=AF.Square), [p_], [t2])
                    q_ = pq[0]
                    P.op("pe", lambda e: e.matmul(q_.t[:, :n], ones.t[:], t2.t[:, :n], start=True, stop=True),
                         reads=[ones.b, t2.b], writes=[q_.b])
                    ew("act", lambda e: e.activation(out=t3.t[:, :n], in_=q_.t[:, :n], func=AF.Sqrt, scale=1.0 / 128,
                                                     bias=epsc.t[:, 0:1]), [q_, epsc], [t3])
                    ew("dve", lambda e: e.reciprocal(out=t3.t[:, :n], in_=t3.t[:, :n]), [t3], [t3])
                    if is_ctx:
                        ew("dve", lambda e: e.scalar_tensor_tensor(out=o16.t[:, :n], in0=p_.t[:, 1:n + 1],
                                                                   scalar=gqg.t[:, gcol:gcol + 1], in1=t3.t[:, :n],
                                                                   op0=ALU.mult, op1=ALU.mult), [p_, gqg, t3], [o16])
                        store(dst_ap, o16)
                        return
                    ew("dve", lambda e: e.scalar_tensor_tensor(out=kx.t[:, :n], in0=p_.t[:, 1:n + 1],
                                                               scalar=gqg.t[:, gcol:gcol + 1], in1=t3.t[:, :n],
                                                               op0=ALU.mult, op1=ALU.mult), [p_, gqg, t3], [kx])
                    q2 = pq[1]
                    P.op("pe", lambda e: e.matmul(q2.t[:, :n], R128, kx.t[:, :n], start=True, stop=True),
                         reads=[cst.b, kx.b], writes=[q2.b])
                    ew("dve", lambda e: e.tensor_tensor(out=t2.t[:, :n], in0=kx.t[:, :n], in1=rope.t[:, 0, :n],
                                                        op=ALU.mult), [kx, rope], [t2])
                    ew("dve", lambda e: e.tensor_tensor(out=t3.t[:, :n], in0=q2.t[:, :n], in1=rope.t[:, 1, :n],
                                                        op=ALU.mult), [q2, rope], [t3])
                    ew("dve", lambda e: e.tensor_tensor(out=o16.t[:, :n], in0=t2.t[:, :n], in1=t3.t[:, :n], op=ALU.add),
                       [t2, t3], [o16])
                    store(dst_ap, o16)

                for hq in range(16):
                    qk_head(54 + hq, 0, qT[hq, :, c0:c0 + n])
                for hk in range(4):
                    qk_head(70 + hk, 1, kT[hk, :, c0:c0 + n])
                for hv in range(4):
                    p_ = proj(ev_w_in, 74 + hv, 128, Wd)
                    ew("act", lambda e: e.activation(out=prv.t[:, :n], in_=p_.t[:, 1:n + 1], func=AF.Copy), [p_], [prv])
                    transpose_store(prv.t, n, lambda t0, m: [(vg_tm[c0 + t0:c0 + t0 + m, hv * 128:(hv + 1) * 128], 0, 128)],
                                    tp, tsb16, cnt, [prv.b])
        P.barrier()

        with ExitStack() as ph:
            TB, TBV, TBY = 64, 4, 1
            FN = ("kk", "b", "w", "kd", "r")
            S = [sb(ph, f"p2_S{d}", [128, 1024]) for d in range(2)]
            Fb = [[{f: sb(ph, f"p2_F{d}{j}{f}", [128, 16, TB]) for f in FN} for j in range(2)] for d in range(2)]
            vbc = [[sb(ph, f"p2_v{d}{j}", [128, TBV, 1024]) for j in range(2)] for d in range(2)]
            tm = [{k_: sb(ph, f"p2_{k_}{d}", [128, 1024]) for k_ in ("t1", "t2", "t3", "t4")} for d in range(2)]
            yst = [[sb(ph, f"p2_y{d}{j}", [2, TBY, 1024]) for j in range(2)] for d in range(2)]
            sa = [ps(ph, f"p2_sa{d}", [128, 1024]) for d in range(2)]
            yp = [ps(ph, f"p2_yp{d}", [128, 1024]) for d in range(2)]
            order = [list(range(T)), list(range(CTX - 1, -1, -1)) + list(range(T - 1, CTX - 1, -1))]
            srcs = [{"kk": sc_kk, "b": sc_b[d], "w": sc_w[d], "kd": sc_kd[d], "r": sc_r} for d in range(2)]
            for d in range(2):
                P.op("dve", lambda e: e.memset(S[d].t[:], 0.0), writes=[S[d].b])

            def v3(ap):
                return ap.rearrange("p (c k) -> p c k", k=64)

            for s in range(T):
                for d in range(2):
                    t = order[d][s]
                    rev = d == 1
                    kb = s // TB
                    vb_i = s // TBV

                    def load_feat(kb_):
                        s0 = kb_ * TB
                        if s0 >= T:
                            return
                        t0_ = order[d][s0]
                        lo_ = (t0_ - TB + 1) if rev else t0_
                        for f in FN:
                            dst = Fb[d][kb_ % 2][f]
                            P.dma("sp", dst.t[:], srcs[d][f][:, :, lo_:lo_ + TB].rearrange("c p t -> p c t"),
                                  writes=[dst.b])

                    def load_v(vb_):
                        s0 = vb_ * TBV
                        if s0 >= T:
                            return
                        t0_ = order[d][s0]
                        tv = (t0_ - TBV + 1) if rev else t0_
                        vt_ = vbc[d][vb_ % 2]
                        for hl in range(2):
                            src = v_sc[hl, tv:tv + TBV].rearrange("t c k -> (t c k)").unsqueeze(0)
                            P.dma("sp", vt_.t[hl * 64:(hl + 1) * 64].rearrange("p t f -> p (t f)"),
                                  src.broadcast_to([64, TBV * 1024]), writes=[vt_.b])

                    if s == 0:
                        load_feat(0)
                        load_v(0)
                    if s % TB == 0:
                        load_feat(kb + 1)
                    if s % TBV == 0:
                        load_v(vb_i + 1)
                    i = (TB - 1 - s % TB) if rev else (s % TB)
                    F = Fb[d][kb % 2]
                    vt = vbc[d][vb_i % 2]
                    iv = (TBV - 1 - s % TBV) if rev else (s % TBV)

                    def bc(f):
                        return F[f].t[:, :, i:i + 1].broadcast_to([128, 16, 64])

                    Sd, t1, t2, t3, t4 = S[d], tm[d]["t1"], tm[d]["t2"], tm[d]["t3"], tm[d]["t4"]
                    P.op("dve", lambda e: e.tensor_tensor(out=v3(t1.t[:]), in0=v3(Sd.t[:]), in1=bc("kk"), op=ALU.mult),
                         reads=[Sd.b, F["kk"].b], writes=[t1.b])
                    for hf in range(2):
                        P.op("pe", lambda e: e.matmul(sa[d].t[:, hf * 512:(hf + 1) * 512], blk64,
                                                      t1.t[:, hf * 512:(hf + 1) * 512], start=True, stop=True),
                             reads=[t1.b, cst.b], writes=[sa[d].b])
                    P.op("dve", lambda e: e.tensor_tensor(out=v3(t2.t[:]), in0=v3(sa[d].t[:]), in1=bc("b"), op=ALU.mult),
                         reads=[sa[d].b, F["b"].b], writes=[t2.b])
                    P.op("dve", lambda e: e.tensor_tensor(out=v3(Sd.t[:]), in0=v3(Sd.t[:]), in1=bc("w"), op=ALU.mult),
                         reads=[Sd.b, F["w"].b], writes=[Sd.b])
                    P.op("dve", lambda e: e.tensor_tensor(out=Sd.t[:], in0=Sd.t[:], in1=t2.t[:], op=ALU.subtract),
                         reads=[Sd.b, t2.b], writes=[Sd.b])
                    P.op("pool", lambda e: e.tensor_tensor(out=v3(t3.t[:]), in0=v3(vt.t[:, iv, :]), in1=bc("kd"),
                                                           op=ALU.mult), reads=[vt.b, F["kd"].b], writes=[t3.b])
                    P.op("dve", lambda e: e.tensor_tensor(out=Sd.t[:], in0=Sd.t[:], in1=t3.t[:], op=ALU.add),
                         reads=[Sd.b, t3.b], writes=[Sd.b])
                    P.op("pool", lambda e: e.tensor_tensor(out=v3(t4.t[:]), in0=v3(Sd.t[:]), in1=bc("r"), op=ALU.mult),
                         reads=[Sd.b, F["r"].b], writes=[t4.b])
                    for hf in range(2):
                        P.op("pe", lambda e: e.matmul(yp[d].t[0:2, hf * 512:(hf + 1) * 512], sel2,
                                                      t4.t[:, hf * 512:(hf + 1) * 512], start=True, stop=True),
                             reads=[t4.b, cst.b], writes=[yp[d].b])
                    yb = s // TBY
                    ys = yst[d][yb % 2]
                    iy = (TBY - 1 - s % TBY) if rev else (s % TBY)
                    P.op("act", lambda e: e.activation(out=ys.t[0:2, iy, :], in_=yp[d].t[0:2, :], func=AF.Copy),
                         reads=[yp[d].b], writes=[ys.b])
                    if s % TBY == TBY - 1:
                        ty = t if rev else (t - TBY + 1)
                        P.dma("sp", y_sc[d][ty:ty + TBY].rearrange("t l c k -> l t (c k)"), ys.t[0:2, :, :], reads=[ys.b])
        P.barrier()

        NKC = T // 128
        lam_init = 0.8 - 0.6 * math.exp(-0.3 * 1)

        def attention_phase(st, jobs, pfx):
            kbuf = sb(st, pfx + "kb", [128, 2, T], BF16)
            vbuf = sb(st, pfx + "vb", [128, NKC, 128], BF16)
            qbuf = [sb(st, pfx + f"qb{i}", [128, 2, 512], BF16) for i in range(2)]
            pex = [sb(st, pfx + f"pe{i}", [128, 512], BF16) for i in range(3)]
            res = [sb(st, pfx + f"rs{i}", [128, 512]) for i in range(3)]
            res16 = [sb(st, pfx + f"rh{i}", [128, 512], BF16) for i in range(3)]
            rd = sb(st, pfx + "rd", [128, 512])
            Sp = [ps(st, pfx + f"S{i}", [128, 512]) for i in range(3)]
            Op = ps(st, pfx + "O", [128, 512])
            Dp = ps(st, pfx + "D", [128, 512])
            ci = [0, 0, 0]
            for jb in jobs:
                if jb.get("newk", True):
                    for j, (ap_, rows, pb) in enumerate(jb["kparts"]):
                        P.dma("sp", kbuf.t[pb:pb + rows, j, :], ap_, writes=[kbuf.b])
                if jb.get("newv", True):
                    P.dma("sp", vbuf.t[:], jb["v"].rearrange("(c p) e -> p c e", p=128), writes=[vbuf.b])
                for (q0, nq, khi) in jb["qtiles"]:
                    qb = qbuf[ci[0] % 2]
                    ci[0] += 1
                    for j, (ap_, rows, pb) in enumerate(jb["qparts"]):
                        P.dma("sp", qb.t[pb:pb + rows, j, :nq], ap_[:, q0:q0 + nq], writes=[qb.b])
                    nk = khi // 128
                    for kc in range(nk):
                        S_ = Sp[ci[1] % 3]
                        px = pex[ci[1] % 3]
                        ci[1] += 1
                        np_ = len(jb["kparts"])
                        for j, (ap_, rows, pb) in enumerate(jb["kparts"]):
                            P.op("pe", lambda e: e.matmul(S_.t[:, :nq], kbuf.t[pb:pb + rows, j, kc * 128:(kc + 1) * 128],
                                                          qb.t[pb:pb + rows, j, :nq], start=(j == 0), stop=(j == np_ - 1)),
                                 reads=[kbuf.b, qb.b], writes=[S_.b])
                        P.op("act", lambda e: e.activation(out=px.t[:, :nq], in_=S_.t[:, :nq], func=AF.Exp,
                                                           scale=jb["scale"]), reads=[S_.b], writes=[px.b])
                        P.op("pe", lambda e: e.matmul(Op.t[:, :nq], vbuf.t[:, kc, :], px.t[:, :nq], start=(kc == 0),
                                                      stop=(kc == nk - 1)), reads=[vbuf.b, px.b], writes=[Op.b])
                        P.op("pe", lambda e: e.matmul(Dp.t[:, :nq], ones16.t[:], px.t[:, :nq], start=(kc == 0),
                                                      stop=(kc == nk - 1)), reads=[ones16.b, px.b], writes=[Dp.b])
                    rs = (res if jb.get("f32res", False) else res16)[ci[2] % 3]
                    ci[2] += 1
                    P.op("dve", lambda e: e.reciprocal(out=rd.t[:, :nq], in_=Dp.t[:, :nq]), reads=[Dp.b], writes=[rd.b])
                    P.op("dve", lambda e: e.tensor_tensor(out=rs.t[:, :nq], in0=Op.t[:, :nq], in1=rd.t[:, :nq],
                                                          op=ALU.mult), reads=[Op.b, rd.b], writes=[rs.b])
                    jb["epi"](rs, q0, nq)

        lat_q = [(CTX + 512 * j, 512, T) for j in range(SEQ // 512)]

        with ExitStack() as ph:
            jobs = []
            for g in range(4):
                for hh in range(4):
                    hq = g * 4 + hh
                    jobs.append(dict(kparts=[(kT[g], 128, 0)], qparts=[(qT[hq], 128, 0)],
                                     v=vg_tm[:, g * 128:(g + 1) * 128], scale=128 ** -0.5,
                                     qtiles=[(0, CTX, CTX)] + lat_q, newk=(hh == 0), newv=(hh == 0),
                                     epi=(lambda rs, q0, nq, hq=hq: P.dma("sp", attT[hq, :, q0:q0 + nq], rs.t[:, :nq],
                                                                           reads=[rs.b]))))
            attention_phase(ph, jobs, "p3_")
        P.barrier()

        def merge_phase(st, l, tiles, cat_builder, W_out, h_in, h_out, final, pfx):
            NT = 512
            G = 16
            cat = sb(st, pfx + "cat", [128, KC, NT], BF16)
            h1 = sb(st, pfx + "h1", [128, KC, NT])
            u = sb(st, pfx + "u", [128, G, NT], BF16)
            slab = [sb(st, pfx + f"sl{i}", [128, KC, 128], BF16) for i in range(3)]
            slab2 = [sb(st, pfx + f"sm{i}", [128, G, 128], BF16) for i in range(3)]
            sq = [sb(st, pfx + f"sq{i}", [128, NT]) for i in range(2)]
            rstd = sb(st, pfx + "rstd", [128, NT])
            pp = [ps(st, pfx + f"pp{i}", [128, 512]) for i in range(3)]
            pss = ps(st, pfx + "pss", [128, 512])
            gi = [0]
            g2i = [0]
            for (c0, n, is_ctx) in tiles:
                r = 1 if is_ctx else 0
                cat_builder(cat, c0, n)
                for half in range(2):
                    P.dma("sp", h1.t[:, half * 16:(half + 1) * 16, :n],
                          h_in[half * 16:(half + 1) * 16, :, c0:c0 + n].rearrange("c p t -> p c t"), writes=[h1.b])
                for oc in range(KC):
                    sl = slab[gi[0] % 3]
                    p_ = pp[gi[0] % 3]
                    gi[0] += 1
                    P.dma("sp", sl.t[:], W_out[oc], writes=[sl.b])
                    for kc in range(KC):
                        P.op("pe", lambda e: e.matmul(p_.t[:, :n], sl.t[:, kc, :], cat.t[:, kc, :n], start=(kc == 0),
                                                      stop=(kc == KC - 1)), reads=[sl.b, cat.b], writes=[p_.b])
                    P.op("dve", lambda e: e.scalar_tensor_tensor(out=h1.t[:, oc, :n], in0=p_.t[:, :n],
                                                                 scalar=mod(l, 2, r, oc), in1=h1.t[:, oc, :n],
                                                                 op0=ALU.mult, op1=ALU.add),
                         reads=[p_.b, modT.b, h1.b], writes=[h1.b])
                rms_mod(h1, n, l, 1, r, (sq, pss, rstd), cat)
                for grp in range(128 // G):
                    for j in range(G):
                        hc = grp * G + j
                        sl = slab[gi[0] % 3]
                        p_ = pp[gi[0] % 3]
                        gi[0] += 1
                        P.dma("sp", sl.t[:], mlp_w1[l, hc], writes=[sl.b])
                        for kc in range(KC):
                            P.op("pe", lambda e: e.matmul(p_.t[:, :n], sl.t[:, kc, :], cat.t[:, kc, :n], start=(kc == 0),
                                                          stop=(kc == KC - 1)), reads=[sl.b, cat.b], writes=[p_.b])
                        s = sq[j % 2]
                        P.op("act", lambda e: e.activation(out=s.t[:, :n], in_=p_.t[:, :n], func=AF.Relu),
                             reads=[p_.b], writes=[s.b])
                        P.op("pool", lambda e: e.tensor_tensor(out=u.t[:, j, :n], in0=s.t[:, :n], in1=s.t[:, :n],
                                                               op=ALU.mult), reads=[s.b], writes=[u.b])
                    for oc in range(KC):
                        s2 = slab2[g2i[0] % 3]
                        p_ = pp[gi[0] % 3]
                        gi[0] += 1
                        g2i[0] += 1
                        P.dma("sp", s2.t[:], mlp_w2[l, oc, :, grp * G:(grp + 1) * G, :], writes=[s2.b])
                        for j in range(G):
                            P.op("pe", lambda e: e.matmul(p_.t[:, :n], s2.t[:, j, :], u.t[:, j, :n], start=(j == 0),
                                                          stop=(j == G - 1)), reads=[s2.b, u.b], writes=[p_.b])
                        P.op("dve", lambda e: e.scalar_tensor_tensor(out=h1.t[:, oc, :n], in0=p_.t[:, :n],
                                                                     scalar=mod(l, 5, r, oc), in1=h1.t[:, oc, :n],
                                                                     op0=ALU.mult, op1=ALU.add),
                             reads=[p_.b, modT.b, h1.b], writes=[h1.b])
                if not final:
                    for half in range(2):
                        P.dma("sp", h_out[half * 16:(half + 1) * 16, :, c0:c0 + n].rearrange("c p t -> p c t"),
                              h1.t[:, half * 16:(half + 1) * 16, :n], reads=[h1.b])
                else:
                    for kc in range(KC):
                        s = sq[kc % 2]
                        P.op("act", lambda e: e.activation(out=s.t[:, :n], in_=h1.t[:, kc, :n], func=AF.Square),
                             reads=[h1.b], writes=[s.b])
                        P.op("pe", lambda e: e.matmul(pss.t[:, :n], ones.t[:], s.t[:, :n], start=(kc == 0),
                                                      stop=(kc == KC - 1)), reads=[s.b, ones.b], writes=[pss.b])
                    P.op("act", lambda e: e.activation(out=rstd.t[:, :n], in_=pss.t[:, :n], func=AF.Sqrt, scale=1.0 / D,
                                                       bias=epsc.t[:, 0:1]), reads=[pss.b, epsc.b], writes=[rstd.b])
                    P.op("dve", lambda e: e.reciprocal(out=rstd.t[:, :n], in_=rstd.t[:, :n]), reads=[rstd.b],
                         writes=[rstd.b])
                    for kc in range(KC):
                        P.op("dve", lambda e: e.scalar_tensor_tensor(out=h1.t[:, kc, :n], in0=h1.t[:, kc, :n],
                                                                     scalar=fg.t[:, kc:kc + 1], in1=rstd.t[:, :n],
                                                                     op0=ALU.mult, op1=ALU.mult),
                             reads=[h1.b, fg.b, rstd.b], writes=[h1.b])
                    for half in range(2):
                        P.dma("sp", outT[half * 16:(half + 1) * 16, :, c0 - CTX:c0 - CTX + n].rearrange("c p t -> p c t"),
                              h1.t[:, half * 16:(half + 1) * 16, :n], reads=[h1.b], is_output=True)

        tilesM = [(0, CTX, True)] + [(CTX + 512 * j, 512, False) for j in range(SEQ // 512)]

        with ExitStack() as ph:
            ya = sb(ph, "p4_ya", [128, 2048])
            yb = sb(ph, "p4_yb", [128, 2048])
            st1 = sb(ph, "p4_st", [128, 4, 32])
            tmp8 = sb(ph, "p4_tmp", [128, 16, 128])
            bon = sb(ph, "p4_bon", [128, 16, 128])
            gg = sb(ph, "p4_g", [128, 16, 128])
            rv4 = sb(ph, "p4_rv", [128, 5, 16])
            tp = [ps(ph, f"p4_tp{i}", [128, 128]) for i in range(2)]
            P.dma("sp", rv4.t[:], rw_vec[:, :, :], writes=[rv4.b])
            tci = [0]

            def cat0(cat, c0, n):
                for blk in range(n // 128):
                    t0 = c0 + blk * 128
                    bs = slice(blk * 128, (blk + 1) * 128)
                    P.dma("sp", ya.t[:], y_sc[0][t0:t0 + 128].rearrange("t l c k -> t (l c k)"), writes=[ya.b])
                    P.dma("sp", yb.t[:], y_sc[1][t0:t0 + 128].rearrange("t l c k -> t (l c k)"), writes=[yb.b])
                    P.dma("sp", bon.t[:], bonT[:, :, t0:t0 + 128].rearrange("c p t -> p c t"), writes=[bon.b])
                    P.dma("sp", gg.t[:], gT[:, :, t0:t0 + 128].rearrange("c p t -> p c t"), writes=[gg.b])
                    P.op("dve", lambda e: e.tensor_tensor(out=ya.t[:], in0=ya.t[:], in1=yb.t[:], op=ALU.add),
                         reads=[ya.b, yb.b], writes=[ya.b])
                    y3 = ya.t[:].rearrange("p (h k) -> p h k", k=64)
                    b3 = yb.t[:].rearrange("p (h k) -> p h k", k=64)
                    P.op("dve", lambda e: e.tensor_reduce(out=st1.t[:, 0, :], in_=y3, axis=AX.X, op=ALU.add),
                         reads=[ya.b], writes=[st1.b])
                    P.op("dve", lambda e: e.tensor_scalar(out=st1.t[:, 0, :], in0=st1.t[:, 0, :], scalar1=1.0 / 64,
                                                          scalar2=None, op0=ALU.mult), reads=[st1.b], writes=[st1.b])
                    P.op("dve", lambda e: e.tensor_tensor(out=y3, in0=y3, in1=st1.t[:, 0, :].unsqueeze(2).broadcast_to(
                        [128, 32, 64]), op=ALU.subtract), reads=[ya.b, st1.b], writes=[ya.b])
                    P.op("act", lambda e: e.activation(out=yb.t[:], in_=ya.t[:], func=AF.Square), reads=[ya.b],
                         writes=[yb.b])
                    P.op("dve", lambda e: e.tensor_reduce(out=st1.t[:, 1, :], in_=b3, axis=AX.X, op=ALU.add),
                         reads=[yb.b], writes=[st1.b])
                    P.op("act", lambda e: e.activation(out=st1.t[:, 2, :], in_=st1.t[:, 1, :], func=AF.Sqrt,
                                                       scale=1.0 / 64, bias=epsc.t[:, 1:2]), reads=[st1.b, epsc.b],
                         writes=[st1.b])
                    P.op("dve", lambda e: e.reciprocal(out=st1.t[:, 3, :], in_=st1.t[:, 2, :]), reads=[st1.b],
                         writes=[st1.b])
                    P.op("dve", lambda e: e.tensor_tensor(out=y3, in0=y3, in1=st1.t[:, 3, :].unsqueeze(2).broadcast_to(
                        [128, 32, 64]), op=ALU.mult), reads=[ya.b, st1.b], writes=[ya.b])
                    y4 = ya.t[:].rearrange("p (l c k) -> p l c k", l=2, c=16)
                    for c in range(16):
                        pt = tp[tci[0] % 2]
                        tci[0] += 1
                        for l_ in range(2):
                            P.op("pe", lambda e: e.matmul(pt.t[l_ * 64:(l_ + 1) * 64, :], y4[:, l_, c, :], ident,
                                                          start=True, stop=True), reads=[ya.b, cst.b], writes=[pt.b])
                        P.op("act", lambda e: e.activation(out=tmp8.t[:, c, :], in_=pt.t[:, :], func=AF.Identity,
                                                           scale=rv4.t[:, 3, c:c + 1], bias=rv4.t[:, 4, c:c + 1]),
                             reads=[pt.b, rv4.b], writes=[tmp8.b])
                    P.op("dve", lambda e: e.tensor_tensor(out=tmp8.t[:], in0=tmp8.t[:], in1=bon.t[:], op=ALU.add),
                         reads=[tmp8.b, bon.b], writes=[tmp8.b])
                    P.op("dve", lambda e: e.tensor_tensor(out=cat.t[:, 0:16, bs], in0=tmp8.t[:], in1=gg.t[:],
                                                          op=ALU.mult), reads=[tmp8.b, gg.b], writes=[cat.b])
                P.dma("sp", cat.t[:, 16:32, :n], attT[:, :, c0:c0 + n].rearrange("c p t -> p c t"), writes=[cat.b])

            merge_phase(ph, 0, tilesM, cat0, ev_w_out, hT0, hT1, False, "p4_")
        P.barrier()

        with ExitStack() as ph:
            WMAX = max(n for _, n, _ in tiles1)
            hs = sb(ph, "p5_hs", [128, KC, WMAX])
            hsb = sb(ph, "p5_hsb", [128, KC, WMAX], BF16)
            sq = [sb(ph, f"p5_sq{i}", [128, WMAX]) for i in range(2)]
            rstd = sb(ph, "p5_rstd", [128, WMAX])
            slab = [sb(ph, f"p5_slab{i}", [128, KC, 128], BF16) for i in range(3)]
            usq = [sb(ph, f"p5_usq{i}", [128, 6, 192], BF16) for i in range(2)]
            usk = [sb(ph, f"p5_usk{i}", [128, 4, 256], BF16) for i in range(2)]
            cq = sb(ph, "p5_cq", [128, 6, WMAX])
            cqb = sb(ph, "p5_cqb", [128, 6, WMAX], BF16)
            ckv = sb(ph, "p5_ckv", [128, 4, WMAX])
            ckvb = sb(ph, "p5_ckvb", [128, 4, WMAX], BF16)
            qn_ = sb(ph, "p5_qn", [128, 6])
            kvn_ = sb(ph, "p5_kvn", [128, 4])
            rope = sb(ph, "p5_rope", [128, 2, WMAX])
            t2 = sb(ph, "p5_t2", [128, WMAX])
            t3 = sb(ph, "p5_t3", [128, WMAX])
            kx = sb(ph, "p5_kx", [128, WMAX])
            oo = [sb(ph, f"p5_o{i}", [128, WMAX], BF16) for i in range(2)]
            pp = [ps(ph, f"p5_pp{i}", [128, 512]) for i in range(2)]
            pq = [ps(ph, f"p5_pq{i}", [128, 512]) for i in range(2)]
            pss = ps(ph, "p5_pss", [128, 512])
            tp = [ps(ph, f"p5_tp{i}", [128, 128]) for i in range(2)]
            tsb = [sb(ph, f"p5_ts{i}", [128, 128], BF16) for i in range(2)]
            P.dma("sp", qn_.t[:], mla_qn[:, :], writes=[qn_.b])
            P.dma("sp", kvn_.t[:], mla_kvn[:, :], writes=[kvn_.b])
            cnt = [0]
            gi = [0]
            oi = [0]
            for (c0, n, is_ctx) in tiles1:
                r = 1 if is_ctx else 0
                for half in range(2):
                    P.dma("sp", hs.t[:, half * 16:(half + 1) * 16, :n],
                          hT1[half * 16:(half + 1) * 16, :, c0:c0 + n].rearrange("c p t -> p c t"), writes=[hs.b])
                if not is_ctx:
                    P.dma("sp", rope.t[:, :, :n], ropes[2:4, :, c0:c0 + n].rearrange("j p t -> p j t"), writes=[rope.b])
                rms_mod(hs, n, 1, 0, r, (sq, pss, rstd), hsb)

                def proj5(g, M):
                    sl = slab[gi[0] % 3]
                    p_ = pp[gi[0] % 2]
                    gi[0] += 1
                    P.dma("sp", sl.t[:], od_w_in[g], writes=[sl.b])
                    for kc in range(KC):
                        P.op("pe", lambda e: e.matmul(p_.t[:M, :n], sl.t[:, kc, :M], hsb.t[:, kc, :n], start=(kc == 0),
                                                      stop=(kc == KC - 1)), reads=[sl.b, hsb.b], writes=[p_.b])
                    return p_

                def rope_store(src_ap, src_b, M, Rm, dst_ap):
                    o_ = oo[oi[0] % 2]
                    oi[0] += 1
                    if is_ctx:
                        P.op("act", lambda e: e.activation(out=o_.t[:M, :n], in_=src_ap, func=AF.Copy), reads=[src_b],
                             writes=[o_.b])
                    else:
                        P.op("act", lambda e: e.activation(out=kx.t[:M, :n], in_=src_ap, func=AF.Copy), reads=[src_b],
                             writes=[kx.b])
                        q2 = pq[1]
                        P.op("pe", lambda e: e.matmul(q2.t[:M, :n], Rm[:M, :M], kx.t[:M, :n], start=True, stop=True),
                             reads=[cst.b, kx.b], writes=[q2.b])
                        P.op("dve", lambda e: e.tensor_tensor(out=t2.t[:M, :n], in0=kx.t[:M, :n], in1=rope.t[:M, 0, :n],
                                                              op=ALU.mult), reads=[kx.b, rope.b], writes=[t2.b])
                        P.op("dve", lambda e: e.tensor_tensor(out=t3.t[:M, :n], in0=q2.t[:M, :n], in1=rope.t[:M, 1, :n],
                                                              op=ALU.mult), reads=[q2.b, rope.b], writes=[t3.b])
                        P.op("dve", lambda e: e.tensor_tensor(out=o_.t[:M, :n], in0=t2.t[:M, :n], in1=t3.t[:M, :n],
                                                              op=ALU.add), reads=[t2.b, t3.b], writes=[o_.b])
                    P.dma("sp", dst_ap, o_.t[:M, :n], reads=[o_.b])

                def lora_norm(dstt, dstb, g0, nch, gains, width):
                    for j in range(nch):
                        p_ = proj5(g0 + j, 128)
                        P.op("act", lambda e: e.activation(out=dstt.t[:, j, :n], in_=p_.t[:, :n], func=AF.Copy),
                             reads=[p_.b], writes=[dstt.b])
                        s = sq[j % 2]
                        P.op("act", lambda e: e.activation(out=s.t[:, :n], in_=p_.t[:, :n], func=AF.Square),
                             reads=[p_.b], writes=[s.b])
                        P.op("pe", lambda e: e.matmul(pss.t[:, :n], ones.t[:], s.t[:, :n], start=(j == 0),
                                                      stop=(j == nch - 1)), reads=[s.b, ones.b], writes=[pss.b])
                    P.op("act", lambda e: e.activation(out=rstd.t[:, :n], in_=pss.t[:, :n], func=AF.Sqrt,
                                                       scale=1.0 / width, bias=epsc.t[:, 0:1]), reads=[pss.b, epsc.b],
                         writes=[rstd.b])
                    P.op("dve", lambda e: e.reciprocal(out=rstd.t[:, :n], in_=rstd.t[:, :n]), reads=[rstd.b],
                         writes=[rstd.b])
                    for j in range(nch):
                        P.op("dve", lambda e: e.scalar_tensor_tensor(out=dstb.t[:, j, :n], in0=dstt.t[:, j, :n],
                                                                     scalar=gains.t[:, j:j + 1], in1=rstd.t[:, :n],
                                                                     op0=ALU.mult, op1=ALU.mult),
                             reads=[dstt.b, gains.b, rstd.b], writes=[dstb.b])

                lora_norm(cq, cqb, 0, 6, qn_, 768)
                lora_norm(ckv, ckvb, 6, 4, kvn_, 512)
                p_ = proj5(10, 64)
                rope_store(p_.t[:64, :n], p_.b, 64, R64, mkp[:, c0:c0 + n])
                ui = 0
                for h in range(16):
                    u_ = usq[ui % 2]
                    ui += 1
                    P.dma("sp", u_.t[:], mla_q_up[h], writes=[u_.b])
                    q_ = pq[0]
                    for j in range(6):
                        P.op("pe", lambda e: e.matmul(q_.t[:, :n], u_.t[:, j, 0:128], cqb.t[:, j, :n], start=(j == 0),
                                                      stop=(j == 5)), reads=[u_.b, cqb.b], writes=[q_.b])
                    o_ = oo[oi[0] % 2]
                    oi[0] += 1
                    P.op("act", lambda e: e.activation(out=o_.t[:, :n], in_=q_.t[:, :n], func=AF.Copy), reads=[q_.b],
                         writes=[o_.b])
                    P.dma("sp", mqn[h, :, c0:c0 + n], o_.t[:, :n], reads=[o_.b])
                    q_ = pq[0]
                    for j in range(6):
                        P.op("pe", lambda e: e.matmul(q_.t[:64, :n], u_.t[:, j, 128:192], cqb.t[:, j, :n], start=(j == 0),
                                                      stop=(j == 5)), reads=[u_.b, cqb.b], writes=[q_.b])
                    rope_store(q_.t[:64, :n], q_.b, 64, R64, mqp[h, :, c0:c0 + n])
                for h in range(16):
                    u_ = usk[ui % 2]
                    ui += 1
                    P.dma("sp", u_.t[:], mla_kv_up[h], writes=[u_.b])
                    for part in range(2):
                        q_ = pq[0]
                        for j in range(4):
                            P.op("pe", lambda e: e.matmul(q_.t[:, :n], u_.t[:, j, part * 128:(part + 1) * 128],
                                                          ckvb.t[:, j, :n], start=(j == 0), stop=(j == 3)),
                                 reads=[u_.b, ckvb.b], writes=[q_.b])
                        if part == 0:
                            o_ = oo[oi[0] % 2]
                            oi[0] += 1
                            P.op("act", lambda e: e.activation(out=o_.t[:, :n], in_=q_.t[:, :n], func=AF.Copy),
                                 reads=[q_.b], writes=[o_.b])
                            P.dma("sp", mkn[h, :, c0:c0 + n], o_.t[:, :n], reads=[o_.b])
                        else:
                            P.op("act", lambda e: e.activation(out=kx.t[:, :n], in_=q_.t[:, :n], func=AF.Copy),
                                 reads=[q_.b], writes=[kx.b])
                            transpose_store(kx.t, n, lambda t0, m: [(mv_tm[c0 + t0:c0 + t0 + m, h * 128:(h + 1) * 128], 0, 128)],
                                            tp, tsb, cnt, [kx.b])
                for h in range(16):
                    p_ = proj5(11 + h, 128)
                    rope_store(p_.t[:, :n], p_.b, 128, R64, dqT[h, :, c0:c0 + n])
                    p_ = proj5(27 + h, 128)
                    rope_store(p_.t[:, :n], p_.b, 128, R64, dkT[h, :, c0:c0 + n])
                    p_ = proj5(43 + h, 128)
                    P.op("act", lambda e: e.activation(out=kx.t[:, :n], in_=p_.t[:, :n], func=AF.Copy), reads=[p_.b],
                         writes=[kx.b])
                    transpose_store(kx.t, n, lambda t0, m: [(dv_tm[c0 + t0:c0 + t0 + m, h * 128:(h + 1) * 128], 0, 128)], tp, tsb,
                                    cnt, [kx.b])
        P.barrier()

        with ExitStack() as ph:
            dl = sb(ph, "p6_dl", [1, 4, 64])
            l1 = sb(ph, "p6_l1", [1, 8])
            lamt = sb(ph, "p6_lam", [128, 2])
            sbl = sb(ph, "p6_sbl", [128, 1])
            keep = sb(ph, "p6_keep", [128, 512])
            dd = sb(ph, "p6_dd", [128, 512])
            dd16 = sb(ph, "p6_ddh", [128, 512], BF16)
            d2 = sb(ph, "p6_d2", [128, 512])
            lp = ps(ph, "p6_lp", [128, 512])
            P.dma("sp", dl.t[:], dlam[:, :, :], writes=[dl.b])
            P.dma("sp", sbl.t[:], subln[:, :], writes=[sbl.b])
            for j in range(2):
                P.op("dve", lambda e: e.tensor_tensor(out=dl.t[:, 2 * j, :], in0=dl.t[:, 2 * j, :], in1=dl.t[:, 2 * j + 1, :],
                                                      op=ALU.mult), reads=[dl.b], writes=[dl.b])
                P.op("dve", lambda e: e.tensor_reduce(out=l1.t[:, j:j + 1], in_=dl.t[:, 2 * j, :], axis=AX.X, op=ALU.add),
                     reads=[dl.b], writes=[l1.b])
            P.op("act", lambda e: e.activation(out=l1.t[:, 2:4], in_=l1.t[:, 0:2], func=AF.Exp), reads=[l1.b],
                 writes=[l1.b])
            P.op("dve", lambda e: e.tensor_tensor(out=l1.t[:, 4:5], in0=l1.t[:, 2:3], in1=l1.t[:, 3:4], op=ALU.subtract),
                 reads=[l1.b], writes=[l1.b])
            P.op("dve", lambda e: e.tensor_scalar(out=l1.t[:, 5:6], in0=l1.t[:, 4:5], scalar1=-1.0, scalar2=-lam_init,
                                                  op0=ALU.mult, op1=ALU.add), reads=[l1.b], writes=[l1.b])
            P.op("pe", lambda e: e.matmul(lp.t[:, 0:1], ones.t[0:1, :], l1.t[0:1, 5:6], start=True, stop=True),
                 reads=[ones.b, l1.b], writes=[lp.b])
            P.op("act", lambda e: e.activation(out=lamt.t[:, 0:1], in_=lp.t[:, 0:1], func=AF.Copy), reads=[lp.b],
                 writes=[lamt.b])
            P.op("dve", lambda e: e.tensor_scalar(out=sbl.t[:], in0=sbl.t[:], scalar1=1.0 - lam_init, scalar2=None,
                                                  op0=ALU.mult), reads=[sbl.b], writes=[sbl.b])
            jobs = []
            for h in range(16):
                jobs.append(dict(kparts=[(mkn[h], 128, 0), (mkp, 64, 0)], qparts=[(mqn[h], 128, 0), (mqp[h], 64, 0)],
                                 v=mv_tm[:, h * 128:(h + 1) * 128], scale=192 ** -0.5, qtiles=lat_q,
                                 epi=(lambda rs, q0, nq, h=h: P.dma("sp", att1[h, :, q0:q0 + nq], rs.t[:, :nq],
                                                                     reads=[rs.b]))))

            def epi_d1(rs, q0, nq):
                P.op("pool", lambda e: e.tensor_copy(out=keep.t[:, :nq], in_=rs.t[:, :nq]), reads=[rs.b], writes=[keep.b])

            def mk_epi_d2(h):
                def epi(rs, q0, nq):
                    P.op("dve", lambda e: e.scalar_tensor_tensor(out=dd.t[:, :nq], in0=rs.t[:, :nq], scalar=lamt.t[:, 0:1],
                                                                 in1=keep.t[:, :nq], op0=ALU.mult, op1=ALU.add),
                         reads=[rs.b, lamt.b, keep.b], writes=[dd.b])
                    P.op("act", lambda e: e.activation(out=d2.t[:, :nq], in_=dd.t[:, :nq], func=AF.Square), reads=[dd.b],
                         writes=[d2.b])
                    P.op("pe", lambda e: e.matmul(lp.t[:, :nq], ones.t[:], d2.t[:, :nq], start=True, stop=True),
                         reads=[ones.b, d2.b], writes=[lp.b])
                    P.op("act", lambda e: e.activation(out=d2.t[:, :nq], in_=lp.t[:, :nq], func=AF.Sqrt, scale=1.0 / 128,
                                                       bias=epsc.t[:, 0:1]), reads=[lp.b, epsc.b], writes=[d2.b])
                    P.op("dve", lambda e: e.reciprocal(out=d2.t[:, :nq], in_=d2.t[:, :nq]), reads=[d2.b], writes=[d2.b])
                    P.op("dve", lambda e: e.scalar_tensor_tensor(out=dd16.t[:, :nq], in0=dd.t[:, :nq], scalar=sbl.t[:, 0:1],
                                                                 in1=d2.t[:, :nq], op0=ALU.mult, op1=ALU.mult),
                         reads=[dd.b, sbl.b, d2.b], writes=[dd16.b])
                    P.dma("sp", att1[16 + h, :, q0:q0 + nq], dd16.t[:, :nq], reads=[dd16.b])
                return epi

            for h in range(16):
                for (q0, nq, khi) in lat_q:
                    for j in range(2):
                        jobs.append(dict(kparts=[(dkT[h][j * 64:(j + 1) * 64], 64, j * 64)],
                                         qparts=[(dqT[h][j * 64:(j + 1) * 64], 64, j * 64)],
                                         v=dv_tm[:, h * 128:(h + 1) * 128], scale=64 ** -0.5, qtiles=[(q0, nq, khi)],
                                         newk=(q0 == lat_q[0][0]), newv=(j == 0 and q0 == lat_q[0][0]), f32res=True,
                                         epi=(epi_d1 if j == 0 else mk_epi_d2(h))))
            attention_phase(ph, jobs, "p6_")
        P.barrier()

        with ExitStack() as ph:
            def cat1(cat, c0, n):
                for half in range(2):
                    P.dma("sp", cat.t[:, half * 16:(half + 1) * 16, :n],
                          att1[half * 16:(half + 1) * 16, :, c0:c0 + n].rearrange("c p t -> p c t"), writes=[cat.b])
            merge_phase(ph, 1, tilesM[1:], cat1, od_w_out, hT1, None, True, "p7_")
        P.barrier()
        P.finish()
    return nc


def slabify(W, groups=None):
    K, N = W.shape
    kc = K // 128
    if groups is None:
        return np.ascontiguousarray(W.reshape(kc, 128, N // 128, 128).transpose(2, 1, 0, 3))
    out = np.zeros((len(groups), 128, kc, 128), np.float32)
    for g, (c0, m) in enumerate(groups):
        out[g, :, :, :m] = W[:, c0:c0 + m].reshape(kc, 128, m).transpose(1, 0, 2)
    return out


EV_GROUPS = ([(c * 128, 128) for c in range(50)] + [(6400 + 96 * j, 96) for j in range(4)]
             + [(RW_IN + c * 128, 128) for c in range(24)])
OD_GROUPS = ([(c * 128, 128) for c in range(10)] + [(1280, 64)] + [(1344 + c * 128, 128) for c in range(48)])


def prep_inputs(b, SEQ, x, c, ctx, c_ctx, ada_w, ada_b, norm1_g, norm2_g, mlp_w1, mlp_w2, final_g,
                ev_w_in, ev_w_out, rw_mu, rw_w0, rw_w_up, rw_a0, rw_a_up, rw_g_up, rw_k_k, rw_k_a,
                rw_r_k, rw_ln_w, rw_ln_b, gq_q_norm, gq_k_norm,
                od_w_in, od_w_out, mla_q_norm, mla_q_up, mla_kv_norm, mla_kv_up,
                diff_lq1, diff_lk1, diff_lq2, diff_lk2, diff_subln, shared=None):
    f = np.float32
    A = np.ascontiguousarray
    T = CTX + SEQ
    m = {}
    h = np.concatenate([ctx[b], x[b]], axis=0)
    m["hT0"] = A(h.T.reshape(KC, 128, T))
    cc = np.stack([c[b], c_ctx], axis=1)
    m["cT"] = A(cc.reshape(KC, 128, 2).transpose(1, 0, 2))
    if shared is not None:
        m.update(shared)
        return m
    sh = {}
    cst = np.zeros((128, 514), f)
    cst[:, 0:128] = np.eye(128, dtype=f)
    cst[0:64, 128:192] = 1.0
    cst[64:128, 192:256] = 1.0
    cst[:, 256:384] = rot_matrix(128, 1)
    cst[:, 384:512] = rot_matrix(64, 2)
    cst[0:64, 512] = 1.0
    cst[64:128, 513] = 1.0
    sh["consts"] = cst
    rp = np.zeros((4, 128, T), f)
    rp[0, :, :CTX] = 1.0
    rp[2, :, :CTX] = 1.0
    c128, s128 = rope_tables(SEQ, 128)
    c64, s64 = rope_tables(SEQ, 64)
    rp[0, :, CTX:] = c128.T
    rp[1, :, CTX:] = s128.T
    rp[2, :, CTX:] = np.concatenate([c64.T, c64.T], 0)
    rp[3, :, CTX:] = np.concatenate([s64.T, s64.T], 0)
    sh["ropes"] = rp
    sh["ada_w"] = A(ada_w.reshape(2, KC, 128, 48, 512).transpose(0, 3, 2, 1, 4))
    sh["ada_bT"] = A(ada_b.reshape(2, 192, 128).transpose(2, 0, 1))
    sh["n1g"] = A(norm1_g.reshape(2, KC, 128).transpose(2, 0, 1))
    sh["n2g"] = A(norm2_g.reshape(2, KC, 128).transpose(2, 0, 1))
    sh["fing"] = A(final_g.reshape(KC, 128).T)
    sh["mlp_w1"] = np.stack([slabify(mlp_w1[l]) for l in range(2)], 0)
    sh["mlp_w2"] = np.stack([slabify(mlp_w2[l]) for l in range(2)], 0)
    sh["ev_w_in"] = slabify(ev_w_in[0], EV_GROUPS)
    sh["ev_w_out"] = slabify(ev_w_out[0])
    mu = np.zeros((128, 57), f)
    mu[:, :53] = rw_mu[0].reshape(53, 128).T
    for j in range(4):
        mu[:96, 53 + j] = rw_mu[0][6400 + 96 * j:6400 + 96 * (j + 1)]
    sh["rw_mu"] = mu
    sh["rw_w0"] = A(rw_w0[0].reshape(2, 16, 128).transpose(2, 0, 1))
    sh["rw_a0"] = A(rw_a0[0].reshape(2, 16, 128).transpose(2, 0, 1))
    sh["rw_w_up"] = A(rw_w_up[0].transpose(1, 0, 2))
    sh["rw_a_up"] = A(rw_a_up[0].transpose(1, 0, 2))
    sh["rw_g_up"] = A(rw_g_up[0].reshape(2, 128, 2048).transpose(1, 0, 2))
    vec = np.stack([rw_k_k[0], rw_k_a[0], rw_r_k[0].reshape(-1), rw_ln_w[0], rw_ln_b[0]], 0)
    sh["rw_vec"] = A(vec.reshape(5, 16, 128).transpose(2, 0, 1))
    sh["gq_g"] = A(np.stack([gq_q_norm[0], gq_k_norm[0]], 1))
    sh["od_w_in"] = slabify(od_w_in[0], OD_GROUPS)
    sh["od_w_out"] = slabify(od_w_out[0])
    sh["mla_qn"] = A(mla_q_norm[0].reshape(6, 128).T)
    sh["mla_kvn"] = A(mla_kv_norm[0].reshape(4, 128).T)
    sh["mla_q_up"] = A(mla_q_up[0].reshape(6, 128, 16, 192).transpose(2, 1, 0, 3))
    sh["mla_kv_up"] = A(mla_kv_up[0].reshape(4, 128, 16, 256).transpose(2, 1, 0, 3))
    sh["dlam"] = A(np.stack([diff_lq1[0], diff_lk1[0], diff_lq2[0], diff_lk2[0]], 0)[None])
    sh["subln"] = A(diff_subln[0].reshape(128, 1))
    sh = {k: v.astype(f) for k, v in sh.items()}
    m.update(sh)
    m["_shared"] = sh
    return m


_NC_CACHE = {}


def kernel(**inputs):
    inputs = {k: np.asarray(v) for k, v in inputs.items()}
    B, SEQ = inputs["x"].shape[0], inputs["x"].shape[1]
    if SEQ not in _NC_CACHE:
        _NC_CACHE[SEQ] = build(SEQ)
    nc = _NC_CACHE[SEQ]
    maps = []
    shared = None
    for b in range(B):
        m = prep_inputs(b, SEQ, shared=shared, **inputs)
        if shared is None:
            shared = m.pop("_shared")
        maps.append(m)
    res = run_bass_kernel_spmd(nc, maps, core_ids=list(range(B)))
    out = np.stack([res.results[b]["outT"].reshape(D, SEQ).T for b in range(B)], 0)
    return np.ascontiguousarray(out.astype(np.float32))
```

```python
import math
from contextlib import ExitStack
import numpy as np
import concourse.bass as bass
import concourse.mybir as mybir
from concourse.bass_utils import run_bass_kernel_spmd

F32 = mybir.dt.float32
BF16 = mybir.dt.bfloat16
AF = mybir.ActivationFunctionType
ALU = mybir.AluOpType
AX = mybir.AxisListType

D = 4096
KC = D // 128
CTX = 256
HID = 4 * D
EPS = 1e-6
GN_EPS = 64e-5
RW_IN = 6784
IN_EVEN = 9856
IN_ODD = 7488
ENGS = ("pe", "act", "dve", "pool", "sp")
NRING = 8


class Buf:
    __slots__ = ("w", "r")

    def __init__(self):
        self.w = None
        self.r = {}


class Prog:
    def __init__(self, nc, sems, rings):
        self.nc = nc
        self.eng = {"pe": nc.tensor, "act": nc.scalar, "dve": nc.vector, "pool": nc.gpsimd, "sp": nc.sync}
        self.cnt = {e: 0 for e in ENGS}
        self.sem = sems
        self.ring = rings
        self.dma_i = {e: 0 for e in ENGS}
        self.waited = {e: {} for e in ENGS}
        self.bufs = {}
        self.out_toks = []
        self.all_dma = {}

    def buf(self, key):
        b = self.bufs.get(key)
        if b is None:
            b = self.bufs[key] = Buf()
        return b

    def _wait(self, e, tok):
        s, v = tok
        w = self.waited[e]
        if w.get(s, 0) >= v:
            return
        w[s] = v
        self.eng[e].wait_ge(s, v)

    def _sync(self, e, reads, writes):
        own = self.sem[e]
        pe = e == "pe"
        for b in reads:
            if b.w is not None and not (pe and b.w[0] is own):
                self._wait(e, b.w)
        for b in writes:
            if b.w is not None and not (pe and b.w[0] is own):
                self._wait(e, b.w)
            for s, v in b.r.items():
                if not (pe and s is own):
                    self._wait(e, (s, v))

    def _mark(self, tok, reads, writes):
        s, v = tok
        for b in reads:
            if b.r.get(s, 0) < v:
                b.r[s] = v
        for b in writes:
            b.w = tok
            b.r = {}

    def op(self, e, fn, reads=(), writes=()):
        self._sync(e, reads, writes)
        self.cnt[e] += 1
        sem = self.sem[e]
        fn(self.eng[e]).then_inc(sem, 1)
        tok = (sem, self.cnt[e])
        self._mark(tok, reads, writes)
        return tok

    def dma(self, e, out, in_, reads=(), writes=(), is_output=False):
        self._sync(e, reads, writes)
        i = self.dma_i[e]
        self.dma_i[e] += 1
        sem = self.ring[e][i % NRING]
        val = 16 * (i // NRING + 1)
        if val > 16:
            self._wait(e, (sem, val - 16))
        self.eng[e].dma_start(out=out, in_=in_).then_inc(sem, 16)
        tok = (sem, val)
        self.all_dma[sem] = val
        self._mark(tok, reads, writes)
        if is_output:
            self.out_toks.append(tok)
        return tok

    def barrier(self):
        for e in ENGS:
            for e2 in ENGS:
                if e2 != e and self.cnt[e2] > 0:
                    self._wait(e, (self.sem[e2], self.cnt[e2]))
            for s, v in self.all_dma.items():
                self._wait(e, (s, v))

    def finish(self):
        for tok in self.out_toks:
            self._wait("sp", tok)


class Tl:
    __slots__ = ("t", "b")

    def __init__(self, t):
        self.t = t
        self.b = Buf()


def rope_tables(seq, rot_dim):
    n_rows = seq // 64
    row = np.repeat(np.arange(n_rows, dtype=np.float32), 64)
    col = np.tile(np.arange(64, dtype=np.float32), n_rows)
    axis_dim = rot_dim // 2
    inv_freq = (np.float32(10000.0) ** (-np.arange(0, axis_dim, 2, dtype=np.float32) / np.float32(axis_dim))).astype(np.float32)
    ang_r = row[:, None] * inv_freq
    ang_c = col[:, None] * inv_freq
    ang = np.concatenate([ang_r, ang_r, ang_c, ang_c], axis=-1).astype(np.float32)
    return np.cos(ang).astype(np.float32), np.sin(ang).astype(np.float32)


def rot_matrix(rot_dim, reps):
    R = np.zeros((128, 128), np.float32)
    half = rot_dim // 2
    q = half // 2
    for rp in range(reps):
        for hb in range(2):
            o = rp * rot_dim + hb * half
            for i in range(q):
                R[o + q + i, o + i] = -1.0
                R[o + i, o + q + i] = 1.0
    return R


def split_tiles(seq, nmax):
    tiles = [(0, CTX, True)]
    nt = -(-seq // nmax)
    base = -(-seq // nt)
    base = -(-base // 8) * 8
    c = 0
    while c < seq:
        n = min(base, seq - c)
        tiles.append((CTX + c, n, False))
        c += n
    return tiles


def build(SEQ, dbg=False):
    T = CTX + SEQ
    nc = bass.Bass("TRN2", target_bir_lowering=False)

    def din(name, shape):
        return nc.dram_tensor(name, list(shape), F32, kind="ExternalInput").ap()

    def dscr(name, shape, dt=F32):
        return nc.dram_tensor(name, list(shape), dt, kind="ExternalOutput" if dbg else "Internal").ap()

    hT0 = din("hT0", [KC, 128, T])
    cT = din("cT", [128, KC, 2])
    consts = din("consts", [128, 4 * 128 + 2])
    ropes = din("ropes", [4, 128, T])
    ada_w = din("ada_w", [2, 48, 128, KC, 512])
    ada_bT = din("ada_bT", [128, 2, 192])
    n1g = din("n1g", [128, 2, KC])
    n2g = din("n2g", [128, 2, KC])
    fing = din("fing", [128, KC])
    mlp_w1_f = din("mlp_w1", [2, 128, 128, KC, 128])
    mlp_w2_f = din("mlp_w2", [2, KC, 128, 128, 128])
    ev_w_in_f = din("ev_w_in", [78, 128, KC, 128])
    ev_w_out_f = din("ev_w_out", [KC, 128, KC, 128])
    rw_mu = din("rw_mu", [128, 57])
    rw_w0 = din("rw_w0", [128, 2, 16])
    rw_a0 = din("rw_a0", [128, 2, 16])
    rw_w_up = din("rw_w_up", [96, 2, 2048])
    rw_a_up = din("rw_a_up", [96, 2, 2048])
    rw_g_up = din("rw_g_up", [128, 2, 2048])
    rw_vec = din("rw_vec", [128, 5, 16])
    gq_g = din("gq_g", [128, 2])
    od_w_in_f = din("od_w_in", [59, 128, KC, 128])
    od_w_out_f = din("od_w_out", [KC, 128, KC, 128])
    mla_qn = din("mla_qn", [128, 6])
    mla_kvn = din("mla_kvn", [128, 4])
    mla_q_up_f = din("mla_q_up", [16, 128, 6, 192])
    mla_kv_up_f = din("mla_kv_up", [16, 128, 4, 256])
    dlam = din("dlam", [1, 4, 64])
    subln = din("subln", [128, 1])
    outT = nc.dram_tensor("outT", [KC, 128, SEQ], F32, kind="ExternalOutput").ap()

    def bfw(name, src):
        return nc.dram_tensor(name, list(src.shape), BF16, kind="Internal").ap()

    mlp_w1 = bfw("mlp_w1_b", mlp_w1_f)
    mlp_w2 = bfw("mlp_w2_b", mlp_w2_f)
    ev_w_in = bfw("ev_w_in_b", ev_w_in_f)
    ev_w_out = bfw("ev_w_out_b", ev_w_out_f)
    od_w_in = bfw("od_w_in_b", od_w_in_f)
    od_w_out = bfw("od_w_out_b", od_w_out_f)
    mla_q_up = bfw("mla_q_up_b", mla_q_up_f)
    mla_kv_up = bfw("mla_kv_up_b", mla_kv_up_f)
    CASTS = [(ev_w_in, ev_w_in_f), (ev_w_out, ev_w_out_f), (mlp_w1, mlp_w1_f), (mlp_w2, mlp_w2_f),
             (od_w_in, od_w_in_f), (mla_q_up, mla_q_up_f), (mla_kv_up, mla_kv_up_f), (od_w_out, od_w_out_f)]

    hT1 = dscr("hT1", [KC, 128, T])
    sc_r = dscr("sc_r", [16, 128, T])
    sc_kk = dscr("sc_kk", [16, 128, T])
    sc_kd = [dscr(f"sc_kd{d}", [16, 128, T]) for d in range(2)]
    sc_b = [dscr(f"sc_b{d}", [16, 128, T]) for d in range(2)]
    sc_w = [dscr(f"sc_w{d}", [16, 128, T]) for d in range(2)]
    gT = dscr("gT", [16, 128, T])
    bonT = dscr("bonT", [16, 128, T])
    v_sc = dscr("v_sc", [2, T, 16, 64])
    y_sc = [dscr(f"y_sc{d}", [T, 2, 16, 64]) for d in range(2)]
    qT = dscr("qT", [16, 128, T], BF16)
    kT = dscr("kT", [4, 128, T], BF16)
    vg_tm = dscr("vg_tm", [T, 512], BF16)
    attT = dscr("attT", [16, 128, T], BF16)
    mqn = dscr("mqn", [16, 128, T], BF16)
    mqp = dscr("mqp", [16, 64, T], BF16)
    mkn = dscr("mkn", [16, 128, T], BF16)
    mkp = dscr("mkp", [64, T], BF16)
    mv_tm = dscr("mv_tm", [T, 2048], BF16)
    dqT = dscr("dqT", [16, 128, T], BF16)
    dkT = dscr("dkT", [16, 128, T], BF16)
    dv_tm = dscr("dv_tm", [T, 2048], BF16)
    att1 = dscr("att1", [32, 128, T], BF16)

    es = ExitStack()
    with es:
        sems = {e: es.enter_context(nc.semaphore("s_" + e)) for e in ENGS}
        rings = {e: [es.enter_context(nc.semaphore(f"r_{e}{i}")) for i in range(NRING)] for e in ENGS}
        P = Prog(nc, sems, rings)

        def sb(st, name, shape, dt=F32):
            return Tl(st.enter_context(nc.sbuf_tensor(name, list(shape), dt)))

        def ps(st, name, shape):
            return Tl(st.enter_context(nc.psum_tensor(name, list(shape), F32)))

        for dstw, srcw in CASTS:
            tot = 1
            for d_ in srcw.shape:
                tot *= d_
            names = " ".join(f"a{i}" for i in range(len(srcw.shape)))
            fs = srcw.rearrange(f"{names} -> ({names})").rearrange("(x y) -> x y", y=2048)
            fd = dstw.rearrange(f"{names} -> ({names})").rearrange("(x y) -> x y", y=2048)
            rows = tot // 2048
            step = 4096
            r0 = 0
            while r0 < rows:
                r1 = min(rows, r0 + step)
                P.dma("pool", fd[r0:r1, :], fs[r0:r1, :], writes=[Buf()])
                r0 = r1

        cst = sb(es, "cst", [128, 4 * 128 + 2])
        ones = sb(es, "ones", [128, 128])
        ones16 = sb(es, "ones16", [128, 128], BF16)
        epsc = sb(es, "epsc", [128, 3])
        modT = sb(es, "modT", [128, 2, 192, 2])
        vecs = sb(es, "vecs", [128, 2, 2, KC])
        fg = sb(es, "fg", [128, KC])
        gsc = sb(es, "gsc", [128, 2, 2, 2, KC])

        P.dma("sp", cst.t[:], consts[:, :], writes=[cst.b])
        P.op("dve", lambda e: e.memset(ones.t[:], 1.0), writes=[ones.b])
        P.op("dve", lambda e: e.memset(ones16.t[:], 1.0), writes=[ones16.b])
        P.op("dve", lambda e: e.memset(epsc.t[:, 0:1], EPS), writes=[epsc.b])
        P.op("dve", lambda e: e.memset(epsc.t[:, 1:2], GN_EPS), writes=[epsc.b])
        P.op("dve", lambda e: e.memset(epsc.t[:, 2:3], 1.0), writes=[epsc.b])
        P.dma("sp", vecs.t[:, :, 0, :], n1g[:, :, :], writes=[vecs.b])
        P.dma("sp", vecs.t[:, :, 1, :], n2g[:, :, :], writes=[vecs.b])
        P.dma("sp", fg.t[:], fing[:, :], writes=[fg.b])
        ident = cst.t[:, 0:128]
        blk64 = cst.t[:, 128:256]
        R128 = cst.t[:, 256:384]
        R64 = cst.t[:, 384:512]
        sel2 = cst.t[:, 512:514]

        with ExitStack() as ph:
            sc = sb(ph, "p0_c", [128, KC, 2])
            ab = sb(ph, "p0_ab", [128, 2, 192])
            slabs = [sb(ph, f"p0_w{i}", [128, KC, 512]) for i in range(2)]
            pm = [ps(ph, f"p0_ps{i}", [128, 8]) for i in range(2)]
            P.dma("sp", sc.t[:], cT[:, :, :], writes=[sc.b])
            P.dma("sp", ab.t[:], ada_bT[:, :, :], writes=[ab.b])
            P.op("act", lambda e: e.activation(out=sc.t[:], in_=sc.t[:], func=AF.Silu), reads=[sc.b], writes=[sc.b])
            i = 0
            for l in range(2):
                for nt in range(48):
                    sl = slabs[i % 2]
                    pp = pm[i % 2]
                    i += 1
                    P.dma("sp", sl.t[:], ada_w[l, nt], writes=[sl.b])
                    for j in range(4):
                        for kc in range(KC):
                            P.op("pe", lambda e: e.matmul(pp.t[:, 2 * j:2 * j + 2], sl.t[:, kc, j * 128:(j + 1) * 128],
                                                          sc.t[:, kc, :], start=(kc == 0), stop=(kc == KC - 1)),
                                 reads=[sl.b, sc.b], writes=[pp.b])
                    P.op("dve", lambda e: e.tensor_tensor(
                        out=modT.t[:, l, nt * 4:(nt + 1) * 4, :], in0=pp.t[:, 0:8].rearrange("p (j r) -> p j r", r=2),
                        in1=ab.t[:, l, nt * 4:(nt + 1) * 4].unsqueeze(2).broadcast_to([128, 4, 2]), op=ALU.add),
                        reads=[pp.b, ab.b], writes=[modT.b])
            for l in range(2):
                for w in range(2):
                    for r in range(2):
                        s0 = (1 + 3 * w) * 32
                        P.op("dve", lambda e: e.scalar_tensor_tensor(
                            out=gsc.t[:, l, w, r, :], in0=modT.t[:, l, s0:s0 + 32, r], scalar=1.0,
                            in1=vecs.t[:, l, w, :], op0=ALU.add, op1=ALU.mult),
                            reads=[modT.b, vecs.b], writes=[gsc.b])
        P.barrier()

        def mod(l, sec, r, c):
            return modT.t[:, l, sec * 32 + c, r:r + 1]

        def rms_mod(src, W, l, which, r, ph_tiles, dst):
            sq, pss, rstd = ph_tiles
            for kc in range(KC):
                s = sq[kc % 2]
                P.op("act", lambda e: e.activation(out=s.t[:, :W], in_=src.t[:, kc, :W], func=AF.Square),
                     reads=[src.b], writes=[s.b])
                P.op("pe", lambda e: e.matmul(pss.t[:, :W], ones.t[:], s.t[:, :W], start=(kc == 0),
                                              stop=(kc == KC - 1)), reads=[s.b, ones.b], writes=[pss.b])
            P.op("act", lambda e: e.activation(out=rstd.t[:, :W], in_=pss.t[:, :W], func=AF.Sqrt, scale=1.0 / D,
                                               bias=epsc.t[:, 0:1]), reads=[pss.b, epsc.b], writes=[rstd.b])
            P.op("dve", lambda e: e.reciprocal(out=rstd.t[:, :W], in_=rstd.t[:, :W]), reads=[rstd.b], writes=[rstd.b])
            for kc in range(KC):
                s = sq[kc % 2]
                P.op("dve", lambda e: e.tensor_tensor(out=s.t[:, :W], in0=src.t[:, kc, :W], in1=rstd.t[:, :W],
                                                      op=ALU.mult), reads=[src.b, rstd.b], writes=[s.b])
                P.op("act", lambda e: e.activation(out=dst.t[:, kc, :W], in_=s.t[:, :W], func=AF.Identity,
                                                   scale=gsc.t[:, l, which, r, kc:kc + 1], bias=mod(l, 3 * which, r, kc)),
                     reads=[s.b, gsc.b, modT.b], writes=[dst.b])

        def transpose_store(src_ap, n, dsts_fn, tp, tsb, cnt, rd):
            t0 = 0
            while t0 < n:
                m = min(128, n - t0)
                pt = tp[cnt[0] % 2]
                ts_ = tsb[cnt[0] % 2]
                cnt[0] += 1
                P.op("pe", lambda e: e.transpose(pt.t[:m, :], src_ap[:, t0:t0 + m], ident), reads=rd + [cst.b],
                     writes=[pt.b])
                P.op("act", lambda e: e.activation(out=ts_.t[:m, :], in_=pt.t[:m, :], func=AF.Copy), reads=[pt.b],
                     writes=[ts_.b])
                for (dap, lo_, hi_) in dsts_fn(t0, m):
                    P.dma("sp", dap, ts_.t[:m, lo_:hi_], reads=[ts_.b])
                t0 += m

        tiles1 = split_tiles(SEQ, 448)
        with ExitStack() as ph:
            WMAX = max(n for _, n, _ in tiles1) + 2
            hs = sb(ph, "p1_hs", [128, KC, WMAX])
            hsb = sb(ph, "p1_hsb", [128, KC, WMAX], BF16)
            sq = [sb(ph, f"p1_sq{i}", [128, WMAX]) for i in range(2)]
            rstd = sb(ph, "p1_rstd", [128, WMAX])
            slab = [sb(ph, f"p1_slab{i}", [128, KC, 128], BF16) for i in range(3)]
            aup = sb(ph, "p1_aup", [96, 2, 2048])
            wup = sb(ph, "p1_wup", [96, 2, 2048])
            gup = sb(ph, "p1_gup", [128, 2, 2048])
            mu = sb(ph, "p1_mu", [128, 57])
            omu = sb(ph, "p1_omu", [128, 57])
            hmu = sb(ph, "p1_hmu", [128, 57])
            w0s = sb(ph, "p1_w0", [128, 2, 16])
            a0s = sb(ph, "p1_a0", [128, 2, 16])
            rv = sb(ph, "p1_rv", [128, 5, 16])
            omka = sb(ph, "p1_omka", [128, 16])
            gqg = sb(ph, "p1_gqg", [128, 2])
            rope = sb(ph, "p1_rope", [128, 2, WMAX])
            psb = sb(ph, "p1_psb", [128, WMAX])
            names = ["t1", "prr", "prk", "prv", "ad0", "ad1", "wd0", "wd1", "sg0", "sg1", "kx", "t2", "t3", "a_0",
                     "a_1", "kk", "o0", "o1", "o2", "o3"]
            tt = {nm: sb(ph, "p1_" + nm, [128, WMAX]) for nm in names}
            pp = [ps(ph, f"p1_pp{i}", [128, 512]) for i in range(2)]
            pq = [ps(ph, f"p1_pq{i}", [128, 512]) for i in range(2)]
            pss = ps(ph, "p1_pss", [128, 512])
            tp = [ps(ph, f"p1_tp{i}", [128, 128]) for i in range(2)]
            tsb = [sb(ph, f"p1_ts{i}", [128, 128]) for i in range(2)]
            tsb16 = [sb(ph, f"p1_tsh{i}", [128, 128], BF16) for i in range(2)]
            ob16 = [sb(ph, f"p1_ob{i}", [128, WMAX], BF16) for i in range(2)]
            obi = [0]
            for dst, src in ((aup, rw_a_up), (wup, rw_w_up), (gup, rw_g_up)):
                P.dma("sp", dst.t[:], src[:, :, :], writes=[dst.b])
            P.dma("sp", mu.t[:], rw_mu[:, :], writes=[mu.b])
            P.dma("sp", w0s.t[:], rw_w0[:, :, :], writes=[w0s.b])
            P.dma("sp", a0s.t[:], rw_a0[:, :, :], writes=[a0s.b])
            P.dma("sp", rv.t[:], rw_vec[:, :, :], writes=[rv.b])
            P.dma("sp", gqg.t[:], gq_g[:, :], writes=[gqg.b])
            P.op("dve", lambda e: e.tensor_scalar(out=omu.t[:], in0=mu.t[:], scalar1=-1.0, scalar2=1.0, op0=ALU.mult,
                                                  op1=ALU.add), reads=[mu.b], writes=[omu.b])
            P.op("dve", lambda e: e.tensor_scalar(out=hmu.t[:], in0=mu.t[:], scalar1=0.5, scalar2=None, op0=ALU.mult),
                 reads=[mu.b], writes=[hmu.b])
            P.op("dve", lambda e: e.tensor_scalar(out=omka.t[:], in0=rv.t[:, 1, :], scalar1=-1.0, scalar2=1.0,
                                                  op0=ALU.mult, op1=ALU.add), reads=[rv.b], writes=[omka.b])
            cnt = [0]
            gi = [0]

            def proj(W_dram, g, M, Wd):
                sl = slab[gi[0] % 3]
                p_ = pp[gi[0] % 2]
                gi[0] += 1
                P.dma("sp", sl.t[:], W_dram[g], writes=[sl.b])
                for kc in range(KC):
                    P.op("pe", lambda e: e.matmul(p_.t[:M, :Wd], sl.t[:, kc, :M], hsb.t[:, kc, :Wd], start=(kc == 0),
                                                  stop=(kc == KC - 1)), reads=[sl.b, hsb.b], writes=[p_.b])
                return p_

            for (c0, n, is_ctx) in tiles1:
                Wd = n + 2
                r = 1 if is_ctx else 0
                s_lo = 0 if is_ctx else CTX
                s_hi = CTX if is_ctx else T
                lo = max(c0 - 1, s_lo)
                hi = min(c0 + n + 1, s_hi)
                left_ok = lo == c0 - 1
                right_ok = hi == c0 + n + 1
                o0_ = lo - (c0 - 1)
                if not left_ok:
                    P.op("dve", lambda e: e.memset(hs.t[:, :, 0:1], 0.0), writes=[hs.b])
                if not right_ok:
                    P.op("dve", lambda e, Wd=Wd: e.memset(hs.t[:, :, Wd - 1:Wd], 0.0), writes=[hs.b])
                for half in range(2):
                    P.dma("sp", hs.t[:, half * 16:(half + 1) * 16, o0_:o0_ + hi - lo],
                          hT0[half * 16:(half + 1) * 16, :, lo:hi].rearrange("c p t -> p c t"), writes=[hs.b])
                if not is_ctx:
                    P.dma("sp", rope.t[:, :, :n], ropes[0:2, :, c0:c0 + n].rearrange("j p t -> p j t"), writes=[rope.b])
                rms_mod(hs, Wd, 0, 0, r, (sq, pss, rstd), hsb)

                def mixed(g, M, mc, dst):
                    p_ = proj(ev_w_in, g, M, Wd)
                    P.op("act", lambda e: e.activation(out=psb.t[:M, :Wd], in_=p_.t[:M, :Wd], func=AF.Copy),
                         reads=[p_.b], writes=[psb.b])
                    if not left_ok:
                        P.op("dve", lambda e: e.memset(psb.t[:M, 0:1], 0.0), writes=[psb.b])
                    if not right_ok:
                        P.op("dve", lambda e: e.memset(psb.t[:M, Wd - 1:Wd], 0.0), writes=[psb.b])
                    t1 = tt["t1"]
                    P.op("dve", lambda e: e.tensor_tensor(out=t1.t[:M, :n], in0=psb.t[:M, 0:n], in1=psb.t[:M, 2:n + 2],
                                                          op=ALU.add), reads=[psb.b], writes=[t1.b])
                    P.op("dve", lambda e: e.tensor_scalar(out=t1.t[:M, :n], in0=t1.t[:M, :n],
                                                          scalar1=hmu.t[:M, mc:mc + 1], scalar2=None, op0=ALU.mult),
                         reads=[t1.b, hmu.b], writes=[t1.b])
                    P.op("dve", lambda e: e.scalar_tensor_tensor(out=dst.t[:M, :n], in0=psb.t[:M, 1:n + 1],
                                                                 scalar=omu.t[:M, mc:mc + 1], in1=t1.t[:M, :n],
                                                                 op0=ALU.mult, op1=ALU.add),
                         reads=[psb.b, t1.b, omu.b], writes=[dst.b])

                def T_(nm):
                    return tt[nm]

                def ew(e_, f, rd, wr):
                    P.op(e_, f, reads=[x.b for x in rd], writes=[x.b for x in wr])

                def store(dst_ap, src, M=128):
                    P.dma("sp", dst_ap, src.t[:M, :n], reads=[src.b])

                for j, nm in enumerate(("wd0", "wd1", "ad0", "ad1")):
                    mixed(50 + j, 96, 53 + j, tt[nm])
                for nm in ("wd0", "wd1"):
                    x_ = tt[nm]
                    ew("act", lambda e: e.activation(out=x_.t[:96, :n], in_=x_.t[:96, :n], func=AF.Tanh), [x_], [x_])
                for j in range(2):
                    x_ = tt[f"sg{j}"]
                    mixed(48 + j, 128, 48 + j, x_)
                    ew("act", lambda e: e.activation(out=x_.t[:, :n], in_=x_.t[:, :n], func=AF.Sigmoid), [x_], [x_])
                prr, prk, prv, kx, t2, t3, kk = (tt[k_] for k_ in ("prr", "prk", "prv", "kx", "t2", "t3", "kk"))
                for c in range(16):
                    cs = slice(c * 128, (c + 1) * 128)
                    mixed(c, 128, c, prr)
                    mixed(16 + c, 128, 16 + c, prk)
                    mixed(32 + c, 128, 32 + c, prv)
                    store(sc_r[c, :, c0:c0 + n], prr)
                    a_ = [tt["a_0"], tt["a_1"]]
                    for d in range(2):
                        q_ = pq[d]
                        adt = tt[f"ad{d}"]
                        P.op("pe", lambda e: e.matmul(q_.t[:, :n], aup.t[:96, d, cs], adt.t[:96, :n], start=True,
                                                      stop=True), reads=[aup.b, adt.b], writes=[q_.b])
                        ew("act", lambda e: e.activation(out=a_[d].t[:, :n], in_=q_.t[:, :n], func=AF.Sigmoid,
                                                         bias=a0s.t[:, d, c:c + 1]), [q_, a0s], [a_[d]])
                    ew("dve", lambda e: e.tensor_scalar(out=kx.t[:, :n], in0=prk.t[:, :n], scalar1=rv.t[:, 0, c:c + 1],
                                                        scalar2=None, op0=ALU.mult), [prk, rv], [kx])
                    ew("act", lambda e: e.activation(out=t2.t[:, :n], in_=kx.t[:, :n], func=AF.Square), [kx], [t2])
                    q_ = pq[0]
                    P.op("pe", lambda e: e.matmul(q_.t[:, :n], blk64, t2.t[:, :n], start=True, stop=True),
                         reads=[cst.b, t2.b], writes=[q_.b])
                    ew("act", lambda e: e.activation(out=t3.t[:, :n], in_=q_.t[:, :n], func=AF.Sqrt), [q_], [t3])
                    ew("dve", lambda e: e.tensor_scalar(out=t3.t[:, :n], in0=t3.t[:, :n], scalar1=1e-12, scalar2=None,
                                                        op0=ALU.max), [t3], [t3])
                    ew("dve", lambda e: e.reciprocal(out=t3.t[:, :n], in_=t3.t[:, :n]), [t3], [t3])
                    ew("dve", lambda e: e.tensor_tensor(out=kk.t[:, :n], in0=kx.t[:, :n], in1=t3.t[:, :n], op=ALU.mult),
                       [kx, t3], [kk])
                    store(sc_kk[c, :, c0:c0 + n], kk)
                    for d in range(2):
                        od, ob = tt[f"o{d}"], tt[f"o{2 + d}"]
                        ew("dve", lambda e: e.tensor_scalar(out=t2.t[:, :n], in0=a_[d].t[:, :n],
                                                            scalar1=rv.t[:, 1, c:c + 1], scalar2=omka.t[:, c:c + 1],
                                                            op0=ALU.mult, op1=ALU.add), [a_[d], rv, omka], [t2])
                        ew("dve", lambda e: e.tensor_tensor(out=od.t[:, :n], in0=prk.t[:, :n], in1=t2.t[:, :n],
                                                            op=ALU.mult), [prk, t2], [od])
                        store(sc_kd[d][c, :, c0:c0 + n], od)
                        ew("dve", lambda e: e.tensor_tensor(out=ob.t[:, :n], in0=kk.t[:, :n], in1=a_[d].t[:, :n],
                                                            op=ALU.mult), [kk, a_[d]], [ob])
                        store(sc_b[d][c, :, c0:c0 + n], ob)
                    ew("dve", lambda e: e.tensor_tensor(out=t2.t[:, :n], in0=tt["o0"].t[:, :n], in1=tt["o1"].t[:, :n],
                                                        op=ALU.add), [tt["o0"], tt["o1"]], [t2])
                    ew("dve", lambda e: e.tensor_tensor(out=t2.t[:, :n], in0=t2.t[:, :n], in1=prr.t[:, :n], op=ALU.mult),
                       [t2, prr], [t2])
                    ew("dve", lambda e: e.tensor_scalar(out=t2.t[:, :n], in0=t2.t[:, :n], scalar1=rv.t[:, 2, c:c + 1],
                                                        scalar2=None, op0=ALU.mult), [t2, rv], [t2])
                    q_ = pq[1]
                    P.op("pe", lambda e: e.matmul(q_.t[:, :n], blk64, t2.t[:, :n], start=True, stop=True),
                         reads=[cst.b, t2.b], writes=[q_.b])
                    ew("dve", lambda e: e.tensor_tensor(out=t3.t[:, :n], in0=q_.t[:, :n], in1=prv.t[:, :n], op=ALU.mult),
                       [q_, prv], [t3])
                    store(bonT[c, :, c0:c0 + n], t3)
                    for d in range(2):
                        q_ = pq[d]
                        wdt = tt[f"wd{d}"]
                        P.op("pe", lambda e: e.matmul(q_.t[:, :n], wup.t[:96, d, cs], wdt.t[:96, :n], start=True,
                                                      stop=True), reads=[wup.b, wdt.b], writes=[q_.b])
                        ew("act", lambda e: e.activation(out=t2.t[:, :n], in_=q_.t[:, :n], func=AF.Sigmoid,
                                                         bias=w0s.t[:, d, c:c + 1]), [q_, w0s], [t2])
                        ow = tt[f"o{d}"]
                        ew("act", lambda e: e.activation(out=ow.t[:, :n], in_=t2.t[:, :n], func=AF.Exp,
                                                         scale=-math.exp(-0.5)), [t2], [ow])
                        store(sc_w[d][c, :, c0:c0 + n], ow)
                    q_ = pq[0]
                    for j in range(2):
                        sgj = tt[f"sg{j}"]
                        P.op("pe", lambda e: e.matmul(q_.t[:, :n], gup.t[:, j, cs], sgj.t[:, :n], start=(j == 0),
                                                      stop=(j == 1)), reads=[gup.b, sgj.b], writes=[q_.b])
                    og = tt["o2"]
                    ew("act", lambda e: e.activation(out=og.t[:, :n], in_=q_.t[:, :n], func=AF.Copy), [q_], [og])
                    store(gT[c, :, c0:c0 + n], og)
                    transpose_store(prv.t, n, lambda t0, m: [(v_sc[l_, c0 + t0:c0 + t0 + m, c, :], l_ * 64, l_ * 64 + 64) for l_ in range(2)],
                                    tp, tsb, cnt, [prv.b])

                def qk_head(g, gcol, dst_ap):
                    p_ = proj(ev_w_in, g, 128, Wd)
                    o16 = ob16[obi[0] % 2]
                    obi[0] += 1
                    ew("act", lambda e: e.activation(out=t2.t[:, :n], in_=p_.t[:, 1:n + 1], func=AF.Square), [p_], [t2])
                    q_ = pq[0]
                    P.op("pe", lambda e: e.matmul(q_.t[:, :n], ones.t[:], t2.t[:, :n], start=True, stop=True),
                         reads=[ones.b, t2.b], writes=[q_.b])
                    ew("act", lambda e: e.activation(out=t3.t[:, :n], in_=q_.t[:, :n], func=AF.Sqrt, scale=1.0 / 128,
                                                     bias=epsc.t[:, 0:1]), [q_, epsc], [t3])
                    ew("dve", lambda e: e.reciprocal(out=t3.t[:, :n], in_=t3.t[:, :n]), [t3], [t3])
                    if is_ctx:
                        ew("dve", lambda e: e.scalar_tensor_tensor(out=o16.t[:, :n], in0=p_.t[:, 1:n + 1],
                                                                   scalar=gqg.t[:, gcol:gcol + 1], in1=t3.t[:, :n],
                                                                   op0=ALU.mult, op1=ALU.mult), [p_, gqg, t3], [o16])
                        store(dst_ap, o16)
                        return
                    ew("dve", lambda e: e.scalar_tensor_tensor(out=kx.t[:, :n], in0=p_.t[:, 1:n + 1],
                                                               scalar=gqg.t[:, gcol:gcol + 1], in1=t3.t[:, :n],
                                                               op0=ALU.mult, op1=ALU.mult), [p_, gqg, t3], [kx])
                    q2 = pq[1]
                    P.op("pe", lambda e: e.matmul(q2.t[:, :n], R128, kx.t[:, :n], start=True, stop=True),
                         reads=[cst.b, kx.b], writes=[q2.b])
                    ew("dve", lambda e: e.tensor_tensor(out=t2.t[:, :n], in0=kx.t[:, :n], in1=rope.t[:, 0, :n],
                                                        op=ALU.mult), [kx, rope], [t2])
                    ew("dve", lambda e: e.tensor_tensor(out=t3.t[:, :n], in0=q2.t[:, :n], in1=rope.t[:, 1, :n],
                                                        op=ALU.mult), [q2, rope], [t3])
                    ew("dve", lambda e: e.tensor_tensor(out=o16.t[:, :n], in0=t2.t[:, :n], in1=t3.t[:, :n], op=ALU.add),
                       [t2, t3], [o16])
                    store(dst_ap, o16)

                for hq in range(16):
                    qk_head(54 + hq, 0, qT[hq, :, c0:c0 + n])
                for hk in range(4):
                    qk_head(70 + hk, 1, kT[hk, :, c0:c0 + n])
                for hv in range(4):
                    p_ = proj(ev_w_in, 74 + hv, 128, Wd)
                    ew("act", lambda e: e.activation(out=prv.t[:, :n], in_=p_.t[:, 1:n + 1], func=AF.Copy), [p_], [prv])
                    transpose_store(prv.t, n, lambda t0, m: [(vg_tm[c0 + t0:c0 + t0 + m, hv * 128:(hv + 1) * 128], 0, 128)],
                                    tp, tsb16, cnt, [prv.b])
        P.barrier()

        with ExitStack() as ph:
            TB, TBV, TBY = 64, 4, 1
            FN = ("kk", "b", "w", "kd", "r")
            S = [sb(ph, f"p2_S{d}", [128, 1024]) for d in range(2)]
            Fb = [[{f: sb(ph, f"p2_F{d}{j}{f}", [128, 16, TB]) for f in FN} for j in range(2)] for d in range(2)]
            vbc = [[sb(ph, f"p2_v{d}{j}", [128, TBV, 1024]) for j in range(2)] for d in range(2)]
            tm = [{k_: sb(ph, f"p2_{k_}{d}", [128, 1024]) for k_ in ("t1", "t2", "t3", "t4")} for d in range(2)]
            yst = [[sb(ph, f"p2_y{d}{j}", [2, TBY, 1024]) for j in range(2)] for d in range(2)]
            sa = [ps(ph, f"p2_sa{d}", [128, 1024]) for d in range(2)]
            yp = [ps(ph, f"p2_yp{d}", [128, 1024]) for d in range(2)]
            order = [list(range(T)), list(range(CTX - 1, -1, -1)) + list(range(T - 1, CTX - 1, -1))]
            srcs = [{"kk": sc_kk, "b": sc_b[d], "w": sc_w[d], "kd": sc_kd[d], "r": sc_r} for d in range(2)]
            for d in range(2):
                P.op("dve", lambda e: e.memset(S[d].t[:], 0.0), writes=[S[d].b])

            def v3(ap):
                return ap.rearrange("p (c k) -> p c k", k=64)

            def load_feat(d, kb_):
                s0 = kb_ * TB
                if s0 >= T:
                    return
                rev = d == 1
                t0_ = order[d][s0]
                lo_ = (t0_ - TB + 1) if rev else t0_
                for f in FN:
                    dst = Fb[d][kb_ % 2][f]
                    P.dma("sp", dst.t[:], srcs[d][f][:, :, lo_:lo_ + TB].rearrange("c p t -> p c t"), writes=[dst.b])

            def load_v(d, vb_):
                s0 = vb_ * TBV
                if s0 >= T:
                    return
                rev = d == 1
                t0_ = order[d][s0]
                tv = (t0_ - TBV + 1) if rev else t0_
                vt_ = vbc[d][vb_ % 2]
                for hl in range(2):
                    src = v_sc[hl, tv:tv + TBV].rearrange("t c k -> (t c k)").unsqueeze(0)
                    P.dma("sp", vt_.t[hl * 64:(hl + 1) * 64].rearrange("p t f -> p (t f)"),
                          src.broadcast_to([64, TBV * 1024]), writes=[vt_.b])

            def stage(s, d, stg):
                t = order[d][s]
                rev = d == 1
                kb = s // TB
                vb_i = s // TBV
                i = (TB - 1 - s % TB) if rev else (s % TB)
                F = Fb[d][kb % 2]
                vt = vbc[d][vb_i % 2]
                iv = (TBV - 1 - s % TBV) if rev else (s % TBV)

                def bc(f):
                    return F[f].t[:, :, i:i + 1].broadcast_to([128, 16, 64])

                Sd, t1, t2, t3, t4 = S[d], tm[d]["t1"], tm[d]["t2"], tm[d]["t3"], tm[d]["t4"]
                if stg == 0:
                    if s == 0:
                        load_feat(d, 0)
                        load_v(d, 0)
                    if s % TB == 0:
                        load_feat(d, kb + 1)
                    if s % TBV == 0:
                        load_v(d, vb_i + 1)
                    P.op("dve", lambda e: e.tensor_tensor(out=v3(t1.t[:]), in0=v3(Sd.t[:]), in1=bc("kk"), op=ALU.mult),
                         reads=[Sd.b, F["kk"].b], writes=[t1.b])
                    for hf in range(2):
                        P.op("pe", lambda e: e.matmul(sa[d].t[:, hf * 512:(hf + 1) * 512], blk64,
                                                      t1.t[:, hf * 512:(hf + 1) * 512], start=True, stop=True),
                             reads=[t1.b, cst.b], writes=[sa[d].b])
                elif stg == 1:
                    P.op("pool", lambda e: e.tensor_tensor(out=v3(Sd.t[:]), in0=v3(Sd.t[:]), in1=bc("w"), op=ALU.mult),
                         reads=[Sd.b, F["w"].b], writes=[Sd.b])
                    P.op("pool", lambda e: e.tensor_tensor(out=v3(t3.t[:]), in0=v3(vt.t[:, iv, :]), in1=bc("kd"),
                                                           op=ALU.mult), reads=[vt.b, F["kd"].b], writes=[t3.b])
                else:
                    P.op("dve", lambda e: e.tensor_tensor(out=v3(t2.t[:]), in0=v3(sa[d].t[:]), in1=bc("b"), op=ALU.mult),
                         reads=[sa[d].b, F["b"].b], writes=[t2.b])
                    P.op("dve", lambda e: e.tensor_tensor(out=Sd.t[:], in0=Sd.t[:], in1=t2.t[:], op=ALU.subtract),
                         reads=[Sd.b, t2.b], writes=[Sd.b])
                    P.op("dve", lambda e: e.tensor_tensor(out=Sd.t[:], in0=Sd.t[:], in1=t3.t[:], op=ALU.add),
                         reads=[Sd.b, t3.b], writes=[Sd.b])
                    P.op("pool", lambda e: e.tensor_tensor(out=v3(t4.t[:]), in0=v3(Sd.t[:]), in1=bc("r"), op=ALU.mult),
                         reads=[Sd.b, F["r"].b], writes=[t4.b])
                    for hf in range(2):
                        P.op("pe", lambda e: e.matmul(yp[d].t[0:2, hf * 512:(hf + 1) * 512], sel2,
                                                      t4.t[:, hf * 512:(hf + 1) * 512], start=True, stop=True),
                             reads=[t4.b, cst.b], writes=[yp[d].b])
                    yb = s // TBY
                    ys = yst[d][yb % 2]
                    iy = (TBY - 1 - s % TBY) if rev else (s % TBY)
                    P.op("act", lambda e: e.activation(out=ys.t[0:2, iy, :], in_=yp[d].t[0:2, :], func=AF.Copy),
                         reads=[yp[d].b], writes=[ys.b])
                    if s % TBY == TBY - 1:
                        ty = t if rev else (t - TBY + 1)
                        P.dma("sp", y_sc[d][ty:ty + TBY].rearrange("t l c k -> l t (c k)"), ys.t[0:2, :, :], reads=[ys.b])

            for s in range(T):
                for stg in range(3):
                    for d in range(2):
                        stage(s, d, stg)
        P.barrier()

        NKC = T // 128
        lam_init = 0.8 - 0.6 * math.exp(-0.3 * 1)

        def attention_phase(st, jobs, pfx):
            kbuf = sb(st, pfx + "kb", [128, 2, T], BF16)
            vbuf = sb(st, pfx + "vb", [128, NKC, 128], BF16)
            qbuf = [sb(st, pfx + f"qb{i}", [128, 2, 512], BF16) for i in range(2)]
            pex = [sb(st, pfx + f"pe{i}", [128, 512], BF16) for i in range(3)]
            res = [sb(st, pfx + f"rs{i}", [128, 512]) for i in range(3)]
            res16 = [sb(st, pfx + f"rh{i}", [128, 512], BF16) for i in range(3)]
            rd = sb(st, pfx + "rd", [128, 512])
            Sp = [ps(st, pfx + f"S{i}", [128, 512]) for i in range(3)]
            Op = ps(st, pfx + "O", [128, 512])
            Dp = ps(st, pfx + "D", [128, 512])
            ci = [0, 0, 0]
            for jb in jobs:
                if jb.get("newk", True):
                    for j, (ap_, rows, pb) in enumerate(jb["kparts"]):
                        P.dma("sp", kbuf.t[pb:pb + rows, j, :], ap_, writes=[kbuf.b])
                if jb.get("newv", True):
                    P.dma("sp", vbuf.t[:], jb["v"].rearrange("(c p) e -> p c e", p=128), writes=[vbuf.b])
                for (q0, nq, khi) in jb["qtiles"]:
                    qb = qbuf[ci[0] % 2]
                    ci[0] += 1
                    for j, (ap_, rows, pb) in enumerate(jb["qparts"]):
                        P.dma("sp", qb.t[pb:pb + rows, j, :nq], ap_[:, q0:q0 + nq], writes=[qb.b])
                    nk = khi // 128
                    for kc in range(nk):
                        S_ = Sp[ci[1] % 3]
                        px = pex[ci[1] % 3]
                        ci[1] += 1
                        np_ = len(jb["kparts"])
                        for j, (ap_, rows, pb) in enumerate(jb["kparts"]):
                            P.op("pe", lambda e: e.matmul(S_.t[:, :nq], kbuf.t[pb:pb + rows, j, kc * 128:(kc + 1) * 128],
                                                          qb.t[pb:pb + rows, j, :nq], start=(j == 0), stop=(j == np_ - 1)),
                                 reads=[kbuf.b, qb.b], writes=[S_.b])
                        P.op("act", lambda e: e.activation(out=px.t[:, :nq], in_=S_.t[:, :nq], func=AF.Exp,
                                                           scale=jb["scale"]), reads=[S_.b], writes=[px.b])
                        P.op("pe", lambda e: e.matmul(Op.t[:, :nq], vbuf.t[:, kc, :], px.t[:, :nq], start=(kc == 0),
                                                      stop=(kc == nk - 1)), reads=[vbuf.b, px.b], writes=[Op.b])
                        P.op("pe", lambda e: e.matmul(Dp.t[:, :nq], ones16.t[:], px.t[:, :nq], start=(kc == 0),
                                                      stop=(kc == nk - 1)), reads=[ones16.b, px.b], writes=[Dp.b])
                    rs = (res if jb.get("f32res", False) else res16)[ci[2] % 3]
                    ci[2] += 1
                    P.op("dve", lambda e: e.reciprocal(out=rd.t[:, :nq], in_=Dp.t[:, :nq]), reads=[Dp.b], writes=[rd.b])
                    P.op("dve", lambda e: e.tensor_tensor(out=rs.t[:, :nq], in0=Op.t[:, :nq], in1=rd.t[:, :nq],
                                                          op=ALU.mult), reads=[Op.b, rd.b], writes=[rs.b])
                    jb["epi"](rs, q0, nq)

        lat_q = [(CTX + 512 * j, 512, T) for j in range(SEQ // 512)]

        with ExitStack() as ph:
            jobs = []
            for g in range(4):
                for hh in range(4):
                    hq = g * 4 + hh
                    jobs.append(dict(kparts=[(kT[g], 128, 0)], qparts=[(qT[hq], 128, 0)],
                                     v=vg_tm[:, g * 128:(g + 1) * 128], scale=128 ** -0.5,
                                     qtiles=[(0, CTX, CTX)] + lat_q, newk=(hh == 0), newv=(hh == 0),
                                     epi=(lambda rs, q0, nq, hq=hq: P.dma("sp", attT[hq, :, q0:q0 + nq], rs.t[:, :nq],
                                                                           reads=[rs.b]))))
            attention_phase(ph, jobs, "p3_")
        P.barrier()

        def merge_phase(st, l, tiles, cat_builder, W_out, h_in, h_out, final, pfx):
            NT = 512
            G = 16
            cat = sb(st, pfx + "cat", [128, KC, NT], BF16)
            h1 = sb(st, pfx + "h1", [128, KC, NT])
            u = sb(st, pfx + "u", [128, G, NT], BF16)
            slab = [sb(st, pfx + f"sl{i}", [128, KC, 128], BF16) for i in range(3)]
            slab2 = [sb(st, pfx + f"sm{i}", [128, G, 128], BF16) for i in range(3)]
            sq = [sb(st, pfx + f"sq{i}", [128, NT]) for i in range(2)]
            rstd = sb(st, pfx + "rstd", [128, NT])
            pp = [ps(st, pfx + f"pp{i}", [128, 512]) for i in range(3)]
            pss = ps(st, pfx + "pss", [128, 512])
            gi = [0]
            g2i = [0]
            for (c0, n, is_ctx) in tiles:
                r = 1 if is_ctx else 0
                cat_builder(cat, c0, n)
                for half in range(2):
                    P.dma("sp", h1.t[:, half * 16:(half + 1) * 16, :n],
                          h_in[half * 16:(half + 1) * 16, :, c0:c0 + n].rearrange("c p t -> p c t"), writes=[h1.b])
                for oc in range(KC):
                    sl = slab[gi[0] % 3]
                    p_ = pp[gi[0] % 3]
                    gi[0] += 1
                    P.dma("sp", sl.t[:], W_out[oc], writes=[sl.b])
                    for kc in range(KC):
                        P.op("pe", lambda e: e.matmul(p_.t[:, :n], sl.t[:, kc, :], cat.t[:, kc, :n], start=(kc == 0),
                                                      stop=(kc == KC - 1)), reads=[sl.b, cat.b], writes=[p_.b])
                    P.op("dve", lambda e: e.scalar_tensor_tensor(out=h1.t[:, oc, :n], in0=p_.t[:, :n],
                                                                 scalar=mod(l, 2, r, oc), in1=h1.t[:, oc, :n],
                                                                 op0=ALU.mult, op1=ALU.add),
                         reads=[p_.b, modT.b, h1.b], writes=[h1.b])
                rms_mod(h1, n, l, 1, r, (sq, pss, rstd), cat)
                for grp in range(128 // G):
                    for j in range(G):
                        hc = grp * G + j
                        sl = slab[gi[0] % 3]
                        p_ = pp[gi[0] % 3]
                        gi[0] += 1
                        P.dma("sp", sl.t[:], mlp_w1[l, hc], writes=[sl.b])
                        for kc in range(KC):
                            P.op("pe", lambda e: e.matmul(p_.t[:, :n], sl.t[:, kc, :], cat.t[:, kc, :n], start=(kc == 0),
                                                          stop=(kc == KC - 1)), reads=[sl.b, cat.b], writes=[p_.b])
                        s = sq[j % 2]
                        P.op("act", lambda e: e.activation(out=s.t[:, :n], in_=p_.t[:, :n], func=AF.Relu),
                             reads=[p_.b], writes=[s.b])
                        P.op("pool", lambda e: e.tensor_tensor(out=u.t[:, j, :n], in0=s.t[:, :n], in1=s.t[:, :n],
                                                               op=ALU.mult), reads=[s.b], writes=[u.b])
                    for oc in range(KC):
                        s2 = slab2[g2i[0] % 3]
                        p_ = pp[gi[0] % 3]
                        gi[0] += 1
                        g2i[0] += 1
                        P.dma("sp", s2.t[:], mlp_w2[l, oc, :, grp * G:(grp + 1) * G, :], writes=[s2.b])
                        for j in range(G):
                            P.op("pe", lambda e: e.matmul(p_.t[:, :n], s2.t[:, j, :], u.t[:, j, :n], start=(j == 0),
                                                          stop=(j == G - 1)), reads=[s2.b, u.b], writes=[p_.b])
                        P.op("dve", lambda e: e.scalar_tensor_tensor(out=h1.t[:, oc, :n], in0=p_.t[:, :n],
                                                                     scalar=mod(l, 5, r, oc), in1=h1.t[:, oc, :n],
                                                                     op0=ALU.mult, op1=ALU.add),
                             reads=[p_.b, modT.b, h1.b], writes=[h1.b])
                if not final:
                    for half in range(2):
                        P.dma("sp", h_out[half * 16:(half + 1) * 16, :, c0:c0 + n].rearrange("c p t -> p c t"),
                              h1.t[:, half * 16:(half + 1) * 16, :n], reads=[h1.b])
                else:
                    for kc in range(KC):
                        s = sq[kc % 2]
                        P.op("act", lambda e: e.activation(out=s.t[:, :n], in_=h1.t[:, kc, :n], func=AF.Square),
                             reads=[h1.b], writes=[s.b])
                        P.op("pe", lambda e: e.matmul(pss.t[:, :n], ones.t[:], s.t[:, :n], start=(kc == 0),
                                                      stop=(kc == KC - 1)), reads=[s.b, ones.b], writes=[pss.b])
                    P.op("act", lambda e: e.activation(out=rstd.t[:, :n], in_=pss.t[:, :n], func=AF.Sqrt, scale=1.0 / D,
                                                       bias=epsc.t[:, 0:1]), reads=[pss.b, epsc.b], writes=[rstd.b])
                    P.op("dve", lambda e: e.reciprocal(out=rstd.t[:, :n], in_=rstd.t[:, :n]), reads=[rstd.b],
                         writes=[rstd.b])
                    for kc in range(KC):
                        P.op("dve", lambda e: e.scalar_tensor_tensor(out=h1.t[:, kc, :n], in0=h1.t[:, kc, :n],
                                                                     scalar=fg.t[:, kc:kc + 1], in1=rstd.t[:, :n],
                                                                     op0=ALU.mult, op1=ALU.mult),
                             reads=[h1.b, fg.b, rstd.b], writes=[h1.b])
                    for half in range(2):
                        P.dma("sp", outT[half * 16:(half + 1) * 16, :, c0 - CTX:c0 - CTX + n].rearrange("c p t -> p c t"),
                              h1.t[:, half * 16:(half + 1) * 16, :n], reads=[h1.b], is_output=True)

        tilesM = [(0, CTX, True)] + [(CTX + 512 * j, 512, False) for j in range(SEQ // 512)]

        with ExitStack() as ph:
            ya = sb(ph, "p4_ya", [128, 2048])
            yb = sb(ph, "p4_yb", [128, 2048])
            st1 = sb(ph, "p4_st", [128, 4, 32])
            tmp8 = sb(ph, "p4_tmp", [128, 16, 128])
            bon = sb(ph, "p4_bon", [128, 16, 128])
            gg = sb(ph, "p4_g", [128, 16, 128])
            rv4 = sb(ph, "p4_rv", [128, 5, 16])
            tp = [ps(ph, f"p4_tp{i}", [128, 128]) for i in range(2)]
            P.dma("sp", rv4.t[:], rw_vec[:, :, :], writes=[rv4.b])
            tci = [0]

            def cat0(cat, c0, n):
                for blk in range(n // 128):
                    t0 = c0 + blk * 128
                    bs = slice(blk * 128, (blk + 1) * 128)
                    P.dma("sp", ya.t[:], y_sc[0][t0:t0 + 128].rearrange("t l c k -> t (l c k)"), writes=[ya.b])
                    P.dma("sp", yb.t[:], y_sc[1][t0:t0 + 128].rearrange("t l c k -> t (l c k)"), writes=[yb.b])
                    P.dma("sp", bon.t[:], bonT[:, :, t0:t0 + 128].rearrange("c p t -> p c t"), writes=[bon.b])
                    P.dma("sp", gg.t[:], gT[:, :, t0:t0 + 128].rearrange("c p t -> p c t"), writes=[gg.b])
                    P.op("dve", lambda e: e.tensor_tensor(out=ya.t[:], in0=ya.t[:], in1=yb.t[:], op=ALU.add),
                         reads=[ya.b, yb.b], writes=[ya.b])
                    y3 = ya.t[:].rearrange("p (h k) -> p h k", k=64)
                    b3 = yb.t[:].rearrange("p (h k) -> p h k", k=64)
                    P.op("dve", lambda e: e.tensor_reduce(out=st1.t[:, 0, :], in_=y3, axis=AX.X, op=ALU.add),
                         reads=[ya.b], writes=[st1.b])
                    P.op("dve", lambda e: e.tensor_scalar(out=st1.t[:, 0, :], in0=st1.t[:, 0, :], scalar1=1.0 / 64,
                                                          scalar2=None, op0=ALU.mult), reads=[st1.b], writes=[st1.b])
                    P.op("dve", lambda e: e.tensor_tensor(out=y3, in0=y3, in1=st1.t[:, 0, :].unsqueeze(2).broadcast_to(
                        [128, 32, 64]), op=ALU.subtract), reads=[ya.b, st1.b], writes=[ya.b])
                    P.op("act", lambda e: e.activation(out=yb.t[:], in_=ya.t[:], func=AF.Square), reads=[ya.b],
                         writes=[yb.b])
                    P.op("dve", lambda e: e.tensor_reduce(out=st1.t[:, 1, :], in_=b3, axis=AX.X, op=ALU.add),
                         reads=[yb.b], writes=[st1.b])
                    P.op("act", lambda e: e.activation(out=st1.t[:, 2, :], in_=st1.t[:, 1, :], func=AF.Sqrt,
                                                       scale=1.0 / 64, bias=epsc.t[:, 1:2]), reads=[st1.b, epsc.b],
                         writes=[st1.b])
                    P.op("dve", lambda e: e.reciprocal(out=st1.t[:, 3, :], in_=st1.t[:, 2, :]), reads=[st1.b],
                         writes=[st1.b])
                    P.op("dve", lambda e: e.tensor_tensor(out=y3, in0=y3, in1=st1.t[:, 3, :].unsqueeze(2).broadcast_to(
                        [128, 32, 64]), op=ALU.mult), reads=[ya.b, st1.b], writes=[ya.b])
                    y4 = ya.t[:].rearrange("p (l c k) -> p l c k", l=2, c=16)
                    for c in range(16):
                        pt = tp[tci[0] % 2]
                        tci[0] += 1
                        for l_ in range(2):
                            P.op("pe", lambda e: e.matmul(pt.t[l_ * 64:(l_ + 1) * 64, :], y4[:, l_, c, :], ident,
                                                          start=True, stop=True), reads=[ya.b, cst.b], writes=[pt.b])
                        P.op("act", lambda e: e.activation(out=tmp8.t[:, c, :], in_=pt.t[:, :], func=AF.Identity,
                                                           scale=rv4.t[:, 3, c:c + 1], bias=rv4.t[:, 4, c:c + 1]),
                             reads=[pt.b, rv4.b], writes=[tmp8.b])
                    P.op("dve", lambda e: e.tensor_tensor(out=tmp8.t[:], in0=tmp8.t[:], in1=bon.t[:], op=ALU.add),
                         reads=[tmp8.b, bon.b], writes=[tmp8.b])
                    P.op("dve", lambda e: e.tensor_tensor(out=cat.t[:, 0:16, bs], in0=tmp8.t[:], in1=gg.t[:],
                                                          op=ALU.mult), reads=[tmp8.b, gg.b], writes=[cat.b])
                P.dma("sp", cat.t[:, 16:32, :n], attT[:, :, c0:c0 + n].rearrange("c p t -> p c t"), writes=[cat.b])

            merge_phase(ph, 0, tilesM, cat0, ev_w_out, hT0, hT1, False, "p4_")
        P.barrier()

        with ExitStack() as ph:
            WMAX = max(n for _, n, _ in tiles1)
            hs = sb(ph, "p5_hs", [128, KC, WMAX])
            hsb = sb(ph, "p5_hsb", [128, KC, WMAX], BF16)
            sq = [sb(ph, f"p5_sq{i}", [128, WMAX]) for i in range(2)]
            rstd = sb(ph, "p5_rstd", [128, WMAX])
            slab = [sb(ph, f"p5_slab{i}", [128, KC, 128], BF16) for i in range(3)]
            usq = [sb(ph, f"p5_usq{i}", [128, 6, 192], BF16) for i in range(2)]
            usk = [sb(ph, f"p5_usk{i}", [128, 4, 256], BF16) for i in range(2)]
            cq = sb(ph, "p5_cq", [128, 6, WMAX])
            cqb = sb(ph, "p5_cqb", [128, 6, WMAX], BF16)
            ckv = sb(ph, "p5_ckv", [128, 4, WMAX])
            ckvb = sb(ph, "p5_ckvb", [128, 4, WMAX], BF16)
            qn_ = sb(ph, "p5_qn", [128, 6])
            kvn_ = sb(ph, "p5_kvn", [128, 4])
            rope = sb(ph, "p5_rope", [128, 2, WMAX])
            t2 = sb(ph, "p5_t2", [128, WMAX])
            t3 = sb(ph, "p5_t3", [128, WMAX])
            kx = sb(ph, "p5_kx", [128, WMAX])
            oo = [sb(ph, f"p5_o{i}", [128, WMAX], BF16) for i in range(2)]
            pp = [ps(ph, f"p5_pp{i}", [128, 512]) for i in range(2)]
            pq = [ps(ph, f"p5_pq{i}", [128, 512]) for i in range(2)]
            pss = ps(ph, "p5_pss", [128, 512])
            tp = [ps(ph, f"p5_tp{i}", [128, 128]) for i in range(2)]
            tsb = [sb(ph, f"p5_ts{i}", [128, 128], BF16) for i in range(2)]
            P.dma("sp", qn_.t[:], mla_qn[:, :], writes=[qn_.b])
            P.dma("sp", kvn_.t[:], mla_kvn[:, :], writes=[kvn_.b])
            cnt = [0]
            gi = [0]
            oi = [0]
            for (c0, n, is_ctx) in tiles1:
                r = 1 if is_ctx else 0
                for half in range(2):
                    P.dma("sp", hs.t[:, half * 16:(half + 1) * 16, :n],
                          hT1[half * 16:(half + 1) * 16, :, c0:c0 + n].rearrange("c p t -> p c t"), writes=[hs.b])
                if not is_ctx:
                    P.dma("sp", rope.t[:, :, :n], ropes[2:4, :, c0:c0 + n].rearrange("j p t -> p j t"), writes=[rope.b])
                rms_mod(hs, n, 1, 0, r, (sq, pss, rstd), hsb)

                def proj5(g, M):
                    sl = slab[gi[0] % 3]
                    p_ = pp[gi[0] % 2]
                    gi[0] += 1
                    P.dma("sp", sl.t[:], od_w_in[g], writes=[sl.b])
                    for kc in range(KC):
                        P.op("pe", lambda e: e.matmul(p_.t[:M, :n], sl.t[:, kc, :M], hsb.t[:, kc, :n], start=(kc == 0),
                                                      stop=(kc == KC - 1)), reads=[sl.b, hsb.b], writes=[p_.b])
                    return p_

                def rope_store(src_ap, src_b, M, Rm, dst_ap):
                    o_ = oo[oi[0] % 2]
                    oi[0] += 1
                    if is_ctx:
                        P.op("act", lambda e: e.activation(out=o_.t[:M, :n], in_=src_ap, func=AF.Copy), reads=[src_b],
                             writes=[o_.b])
                    else:
                        P.op("act", lambda e: e.activation(out=kx.t[:M, :n], in_=src_ap, func=AF.Copy), reads=[src_b],
                             writes=[kx.b])
                        q2 = pq[1]
                        P.op("pe", lambda e: e.matmul(q2.t[:M, :n], Rm[:M, :M], kx.t[:M, :n], start=True, stop=True),
                             reads=[cst.b, kx.b], writes=[q2.b])
                        P.op("dve", lambda e: e.tensor_tensor(out=t2.t[:M, :n], in0=kx.t[:M, :n], in1=rope.t[:M, 0, :n],
                                                              op=ALU.mult), reads=[kx.b, rope.b], writes=[t2.b])
                        P.op("dve", lambda e: e.tensor_tensor(out=t3.t[:M, :n], in0=q2.t[:M, :n], in1=rope.t[:M, 1, :n],
                                                              op=ALU.mult), reads=[q2.b, rope.b], writes=[t3.b])
                        P.op("dve", lambda e: e.tensor_tensor(out=o_.t[:M, :n], in0=t2.t[:M, :n], in1=t3.t[:M, :n],
                                                              op=ALU.add), reads=[t2.b, t3.b], writes=[o_.b])
                    P.dma("sp", dst_ap, o_.t[:M, :n], reads=[o_.b])

                def lora_norm(dstt, dstb, g0, nch, gains, width):
                    for j in range(nch):
                        p_ = proj5(g0 + j, 128)
                        P.op("act", lambda e: e.activation(out=dstt.t[:, j, :n], in_=p_.t[:, :n], func=AF.Copy),
                             reads=[p_.b], writes=[dstt.b])
                        s = sq[j % 2]
                        P.op("act", lambda e: e.activation(out=s.t[:, :n], in_=p_.t[:, :n], func=AF.Square),
                             reads=[p_.b], writes=[s.b])
                        P.op("pe", lambda e: e.matmul(pss.t[:, :n], ones.t[:], s.t[:, :n], start=(j == 0),
                                                      stop=(j == nch - 1)), reads=[s.b, ones.b], writes=[pss.b])
                    P.op("act", lambda e: e.activation(out=rstd.t[:, :n], in_=pss.t[:, :n], func=AF.Sqrt,
                                                       scale=1.0 / width, bias=epsc.t[:, 0:1]), reads=[pss.b, epsc.b],
                         writes=[rstd.b])
                    P.op("dve", lambda e: e.reciprocal(out=rstd.t[:, :n], in_=rstd.t[:, :n]), reads=[rstd.b],
                         writes=[rstd.b])
                    for j in range(nch):
                        P.op("dve", lambda e: e.scalar_tensor_tensor(out=dstb.t[:, j, :n], in0=dstt.t[:, j, :n],
                                                                     scalar=gains.t[:, j:j + 1], in1=rstd.t[:, :n],
                                                                     op0=ALU.mult, op1=ALU.mult),
                             reads=[dstt.b, gains.b, rstd.b], writes=[dstb.b])

                lora_norm(cq, cqb, 0, 6, qn_, 768)
                lora_norm(ckv, ckvb, 6, 4, kvn_, 512)
                p_ = proj5(10, 64)
                rope_store(p_.t[:64, :n], p_.b, 64, R64, mkp[:, c0:c0 + n])
                ui = 0
                for h in range(16):
                    u_ = usq[ui % 2]
                    ui += 1
                    P.dma("sp", u_.t[:], mla_q_up[h], writes=[u_.b])
                    q_ = pq[0]
                    for j in range(6):
                        P.op("pe", lambda e: e.matmul(q_.t[:, :n], u_.t[:, j, 0:128], cqb.t[:, j, :n], start=(j == 0),
                                                      stop=(j == 5)), reads=[u_.b, cqb.b], writes=[q_.b])
                    o_ = oo[oi[0] % 2]
                    oi[0] += 1
                    P.op("act", lambda e: e.activation(out=o_.t[:, :n], in_=q_.t[:, :n], func=AF.Copy), reads=[q_.b],
                         writes=[o_.b])
                    P.dma("sp", mqn[h, :, c0:c0 + n], o_.t[:, :n], reads=[o_.b])
                    q_ = pq[0]
                    for j in range(6):
                        P.op("pe", lambda e: e.matmul(q_.t[:64, :n], u_.t[:, j, 128:192], cqb.t[:, j, :n], start=(j == 0),
                                                      stop=(j == 5)), reads=[u_.b, cqb.b], writes=[q_.b])
                    rope_store(q_.t[:64, :n], q_.b, 64, R64, mqp[h, :, c0:c0 + n])
                for h in range(16):
                    u_ = usk[ui % 2]
                    ui += 1
                    P.dma("sp", u_.t[:], mla_kv_up[h], writes=[u_.b])
                    for part in range(2):
                        q_ = pq[0]
                        for j in range(4):
                            P.op("pe", lambda e: e.matmul(q_.t[:, :n], u_.t[:, j, part * 128:(part + 1) * 128],
                                                          ckvb.t[:, j, :n], start=(j == 0), stop=(j == 3)),
                                 reads=[u_.b, ckvb.b], writes=[q_.b])
                        if part == 0:
                            o_ = oo[oi[0] % 2]
                            oi[0] += 1
                            P.op("act", lambda e: e.activation(out=o_.t[:, :n], in_=q_.t[:, :n], func=AF.Copy),
                                 reads=[q_.b], writes=[o_.b])
                            P.dma("sp", mkn[h, :, c0:c0 + n], o_.t[:, :n], reads=[o_.b])
                        else:
                            P.op("act", lambda e: e.activation(out=kx.t[:, :n], in_=q_.t[:, :n], func=AF.Copy),
                                 reads=[q_.b], writes=[kx.b])
                            transpose_store(kx.t, n, lambda t0, m: [(mv_tm[c0 + t0:c0 + t0 + m, h * 128:(h + 1) * 128], 0, 128)],
                                            tp, tsb, cnt, [kx.b])
                for h in range(16):
                    p_ = proj5(11 + h, 128)
                    rope_store(p_.t[:, :n], p_.b, 128, R64, dqT[h, :, c0:c0 + n])
                    p_ = proj5(27 + h, 128)
                    rope_store(p_.t[:, :n], p_.b, 128, R64, dkT[h, :, c0:c0 + n])
                    p_ = proj5(43 + h, 128)
                    P.op("act", lambda e: e.activation(out=kx.t[:, :n], in_=p_.t[:, :n], func=AF.Copy), reads=[p_.b],
                         writes=[kx.b])
                    transpose_store(kx.t, n, lambda t0, m: [(dv_tm[c0 + t0:c0 + t0 + m, h * 128:(h + 1) * 128], 0, 128)], tp, tsb,
                                    cnt, [kx.b])
        P.barrier()

        with ExitStack() as ph:
            dl = sb(ph, "p6_dl", [1, 4, 64])
            l1 = sb(ph, "p6_l1", [1, 8])
            lamt = sb(ph, "p6_lam", [128, 2])
            sbl = sb(ph, "p6_sbl", [128, 1])
            keep = sb(ph, "p6_keep", [128, 512])
            dd = sb(ph, "p6_dd", [128, 512])
            dd16 = sb(ph, "p6_ddh", [128, 512], BF16)
            d2 = sb(ph, "p6_d2", [128, 512])
            lp = ps(ph, "p6_lp", [128, 512])
            P.dma("sp", dl.t[:], dlam[:, :, :], writes=[dl.b])
            P.dma("sp", sbl.t[:], subln[:, :], writes=[sbl.b])
            for j in range(2):
                P.op("dve", lambda e: e.tensor_tensor(out=dl.t[:, 2 * j, :], in0=dl.t[:, 2 * j, :], in1=dl.t[:, 2 * j + 1, :],
                                                      op=ALU.mult), reads=[dl.b], writes=[dl.b])
                P.op("dve", lambda e: e.tensor_reduce(out=l1.t[:, j:j + 1], in_=dl.t[:, 2 * j, :], axis=AX.X, op=ALU.add),
                     reads=[dl.b], writes=[l1.b])
            P.op("act", lambda e: e.activation(out=l1.t[:, 2:4], in_=l1.t[:, 0:2], func=AF.Exp), reads=[l1.b],
                 writes=[l1.b])
            P.op("dve", lambda e: e.tensor_tensor(out=l1.t[:, 4:5], in0=l1.t[:, 2:3], in1=l1.t[:, 3:4], op=ALU.subtract),
                 reads=[l1.b], writes=[l1.b])
            P.op("dve", lambda e: e.tensor_scalar(out=l1.t[:, 5:6], in0=l1.t[:, 4:5], scalar1=-1.0, scalar2=-lam_init,
                                                  op0=ALU.mult, op1=ALU.add), reads=[l1.b], writes=[l1.b])
            P.op("pe", lambda e: e.matmul(lp.t[:, 0:1], ones.t[0:1, :], l1.t[0:1, 5:6], start=True, stop=True),
                 reads=[ones.b, l1.b], writes=[lp.b])
            P.op("act", lambda e: e.activation(out=lamt.t[:, 0:1], in_=lp.t[:, 0:1], func=AF.Copy), reads=[lp.b],
                 writes=[lamt.b])
            P.op("dve", lambda e: e.tensor_scalar(out=sbl.t[:], in0=sbl.t[:], scalar1=1.0 - lam_init, scalar2=None,
                                                  op0=ALU.mult), reads=[sbl.b], writes=[sbl.b])
            jobs = []
            for h in range(16):
                jobs.append(dict(kparts=[(mkn[h], 128, 0), (mkp, 64, 0)], qparts=[(mqn[h], 128, 0), (mqp[h], 64, 0)],
                                 v=mv_tm[:, h * 128:(h + 1) * 128], scale=192 ** -0.5, qtiles=lat_q,
                                 epi=(lambda rs, q0, nq, h=h: P.dma("sp", att1[h, :, q0:q0 + nq], rs.t[:, :nq],
                                                                     reads=[rs.b]))))

            def epi_d1(rs, q0, nq):
                P.op("pool", lambda e: e.tensor_copy(out=keep.t[:, :nq], in_=rs.t[:, :nq]), reads=[rs.b], writes=[keep.b])

            def mk_epi_d2(h):
                def epi(rs, q0, nq):
                    P.op("dve", lambda e: e.scalar_tensor_tensor(out=dd.t[:, :nq], in0=rs.t[:, :nq], scalar=lamt.t[:, 0:1],
                                                                 in1=keep.t[:, :nq], op0=ALU.mult, op1=ALU.add),
                         reads=[rs.b, lamt.b, keep.b], writes=[dd.b])
                    P.op("act", lambda e: e.activation(out=d2.t[:, :nq], in_=dd.t[:, :nq], func=AF.Square), reads=[dd.b],
                         writes=[d2.b])
                    P.op("pe", lambda e: e.matmul(lp.t[:, :nq], ones.t[:], d2.t[:, :nq], start=True, stop=True),
                         reads=[ones.b, d2.b], writes=[lp.b])
                    P.op("act", lambda e: e.activation(out=d2.t[:, :nq], in_=lp.t[:, :nq], func=AF.Sqrt, scale=1.0 / 128,
                                                       bias=epsc.t[:, 0:1]), reads=[lp.b, epsc.b], writes=[d2.b])
                    P.op("dve", lambda e: e.reciprocal(out=d2.t[:, :nq], in_=d2.t[:, :nq]), reads=[d2.b], writes=[d2.b])
                    P.op("dve", lambda e: e.scalar_tensor_tensor(out=dd16.t[:, :nq], in0=dd.t[:, :nq], scalar=sbl.t[:, 0:1],
                                                                 in1=d2.t[:, :nq], op0=ALU.mult, op1=ALU.mult),
                         reads=[dd.b, sbl.b, d2.b], writes=[dd16.b])
                    P.dma("sp", att1[16 + h, :, q0:q0 + nq], dd16.t[:, :nq], reads=[dd16.b])
                return epi

            for h in range(16):
                for (q0, nq, khi) in lat_q:
                    for j in range(2):
                        jobs.append(dict(kparts=[(dkT[h][j * 64:(j + 1) * 64], 64, j * 64)],
                                         qparts=[(dqT[h][j * 64:(j + 1) * 64], 64, j * 64)],
                                         v=dv_tm[:, h * 128:(h + 1) * 128], scale=64 ** -0.5, qtiles=[(q0, nq, khi)],
                                         newk=(q0 == lat_q[0][0]), newv=(j == 0 and q0 == lat_q[0][0]), f32res=True,
                                         epi=(epi_d1 if j == 0 else mk_epi_d2(h))))
            attention_phase(ph, jobs, "p6_")
        P.barrier()

        with ExitStack() as ph:
            def cat1(cat, c0, n):
                for half in range(2):
                    P.dma("sp", cat.t[:, half * 16:(half + 1) * 16, :n],
                          att1[half * 16:(half + 1) * 16, :, c0:c0 + n].rearrange("c p t -> p c t"), writes=[cat.b])
            merge_phase(ph, 1, tilesM[1:], cat1, od_w_out, hT1, None, True, "p7_")
        P.barrier()
        P.finish()
    return nc


def slabify(W, groups=None):
    K, N = W.shape
    kc = K // 128
    if groups is None:
        return np.ascontiguousarray(W.reshape(kc, 128, N // 128, 128).transpose(2, 1, 0, 3))
    out = np.zeros((len(groups), 128, kc, 128), np.float32)
    for g, (c0, m) in enumerate(groups):
        out[g, :, :, :m] = W[:, c0:c0 + m].reshape(kc, 128, m).transpose(1, 0, 2)
    return out


EV_GROUPS = ([(c * 128, 128) for c in range(50)] + [(6400 + 96 * j, 96) for j in range(4)]
             + [(RW_IN + c * 128, 128) for c in range(24)])
OD_GROUPS = ([(c * 128, 128) for c in range(10)] + [(1280, 64)] + [(1344 + c * 128, 128) for c in range(48)])


def prep_inputs(b, SEQ, x, c, ctx, c_ctx, ada_w, ada_b, norm1_g, norm2_g, mlp_w1, mlp_w2, final_g,
                ev_w_in, ev_w_out, rw_mu, rw_w0, rw_w_up, rw_a0, rw_a_up, rw_g_up, rw_k_k, rw_k_a,
                rw_r_k, rw_ln_w, rw_ln_b, gq_q_norm, gq_k_norm,
                od_w_in, od_w_out, mla_q_norm, mla_q_up, mla_kv_norm, mla_kv_up,
                diff_lq1, diff_lk1, diff_lq2, diff_lk2, diff_subln, shared=None):
    f = np.float32
    A = np.ascontiguousarray
    T = CTX + SEQ
    m = {}
    h = np.concatenate([ctx[b], x[b]], axis=0)
    m["hT0"] = A(h.T.reshape(KC, 128, T))
    cc = np.stack([c[b], c_ctx], axis=1)
    m["cT"] = A(cc.reshape(KC, 128, 2).transpose(1, 0, 2))
    if shared is not None:
        m.update(shared)
        return m
    sh = {}
    cst = np.zeros((128, 514), f)
    cst[:, 0:128] = np.eye(128, dtype=f)
    cst[0:64, 128:192] = 1.0
    cst[64:128, 192:256] = 1.0
    cst[:, 256:384] = rot_matrix(128, 1)
    cst[:, 384:512] = rot_matrix(64, 2)
    cst[0:64, 512] = 1.0
    cst[64:128, 513] = 1.0
    sh["consts"] = cst
    rp = np.zeros((4, 128, T), f)
    rp[0, :, :CTX] = 1.0
    rp[2, :, :CTX] = 1.0
    c128, s128 = rope_tables(SEQ, 128)
    c64, s64 = rope_tables(SEQ, 64)
    rp[0, :, CTX:] = c128.T
    rp[1, :, CTX:] = s128.T
    rp[2, :, CTX:] = np.concatenate([c64.T, c64.T], 0)
    rp[3, :, CTX:] = np.concatenate([s64.T, s64.T], 0)
    sh["ropes"] = rp
    sh["ada_w"] = A(ada_w.reshape(2, KC, 128, 48, 512).transpose(0, 3, 2, 1, 4))
    sh["ada_bT"] = A(ada_b.reshape(2, 192, 128).transpose(2, 0, 1))
    sh["n1g"] = A(norm1_g.reshape(2, KC, 128).transpose(2, 0, 1))
    sh["n2g"] = A(norm2_g.reshape(2, KC, 128).transpose(2, 0, 1))
    sh["fing"] = A(final_g.reshape(KC, 128).T)
    sh["mlp_w1"] = np.stack([slabify(mlp_w1[l]) for l in range(2)], 0)
    sh["mlp_w2"] = np.stack([slabify(mlp_w2[l]) for l in range(2)], 0)
    sh["ev_w_in"] = slabify(ev_w_in[0], EV_GROUPS)
    sh["ev_w_out"] = slabify(ev_w_out[0])
    mu = np.zeros((128, 57), f)
    mu[:, :53] = rw_mu[0].reshape(53, 128).T
    for j in range(4):
        mu[:96, 53 + j] = rw_mu[0][6400 + 96 * j:6400 + 96 * (j + 1)]
    sh["rw_mu"] = mu
    sh["rw_w0"] = A(rw_w0[0].reshape(2, 16, 128).transpose(2, 0, 1))
    sh["rw_a0"] = A(rw_a0[0].reshape(2, 16, 128).transpose(2, 0, 1))
    sh["rw_w_up"] = A(rw_w_up[0].transpose(1, 0, 2))
    sh["rw_a_up"] = A(rw_a_up[0].transpose(1, 0, 2))
    sh["rw_g_up"] = A(rw_g_up[0].reshape(2, 128, 2048).transpose(1, 0, 2))
    vec = np.stack([rw_k_k[0], rw_k_a[0], rw_r_k[0].reshape(-1), rw_ln_w[0], rw_ln_b[0]], 0)
    sh["rw_vec"] = A(vec.reshape(5, 16, 128).transpose(2, 0, 1))
    sh["gq_g"] = A(np.stack([gq_q_norm[0], gq_k_norm[0]], 1))
    sh["od_w_in"] = slabify(od_w_in[0], OD_GROUPS)
    sh["od_w_out"] = slabify(od_w_out[0])
    sh["mla_qn"] = A(mla_q_norm[0].reshape(6, 128).T)
    sh["mla_kvn"] = A(mla_kv_norm[0].reshape(4, 128).T)
    sh["mla_q_up"] = A(mla_q_up[0].reshape(6, 128, 16, 192).transpose(2, 1, 0, 3))
    sh["mla_kv_up"] = A(mla_kv_up[0].reshape(4, 128, 16, 256).transpose(2, 1, 0, 3))
    sh["dlam"] = A(np.stack([diff_lq1[0], diff_lk1[0], diff_lq2[0], diff_lk2[0]], 0)[None])
    sh["subln"] = A(diff_subln[0].reshape(128, 1))
    sh = {k: v.astype(f) for k, v in sh.items()}
    m.update(sh)
    m["_shared"] = sh
    return m


_NC_CACHE = {}


def kernel(**inputs):
    inputs = {k: np.asarray(v) for k, v in inputs.items()}
    B, SEQ = inputs["x"].shape[0], inputs["x"].shape[1]
    if SEQ not in _NC_CACHE:
        _NC_CACHE[SEQ] = build(SEQ)
    nc = _NC_CACHE[SEQ]
    maps = []
    shared = None
    for b in range(B):
        m = prep_inputs(b, SEQ, shared=shared, **inputs)
        if shared is None:
            shared = m.pop("_shared")
        maps.append(m)
    res = run_bass_kernel_spmd(nc, maps, core_ids=list(range(B)))
    out = np.stack([res.results[b]["outT"].reshape(D, SEQ).T for b in range(B)], 0)
    return np.ascontiguousarray(out.astype(np.float32))
```
